# Optimizing a Trainium2 kernel written in Bass

```python
import math
import jax, jax.numpy as jnp
from jax import lax
import numpy as np

D_MODEL = 2048
BATCH = 2
SEQ = 16384
DEPTH = 2

HEAD_DIM = 128
ROPE_THETA = 10000.0
EPS = 1e-6
F32 = jnp.float32

A_HEADS = 4
MOBA_BLOCK = 256
MOBA_TOPK = 3
MOBA_QCHUNK = 64

B_GROUPS = ((128, 1), (512, 4), (2048, 16))
B_HEADS = 4
B_QBLOCK = 128
B_PAD = B_QBLOCK * max(d for _, d in B_GROUPS)

C_HEADS = 8
Q_LORA = 1536
KV_LORA = 512
NOPE_DIM = 128
ROPE_DIM = 64
V_DIM = 128
C_QBLOCK = 128

MEM_LEN = 256
X_HEADS = 4

D_FF = 5504
CONV_W = 3

N_BRANCH = 3
A_W = A_HEADS * HEAD_DIM
B_QKV_W = len(B_GROUPS) * B_HEADS * HEAD_DIM
B_W = B_HEADS * HEAD_DIM
C_W = C_HEADS * V_DIM
X_W = X_HEADS * HEAD_DIM
IN_WIDTHS = (A_W, A_W, A_W, B_QKV_W, B_QKV_W, B_QKV_W, Q_LORA, KV_LORA, ROPE_DIM, N_BRANCH * D_MODEL)
D_IN = sum(IN_WIDTHS)
IN_OFFSETS = tuple(int(o) for o in np.cumsum(IN_WIDTHS)[:-1])

kernel_name = 'hybrid_moba_dilated_mla_block'


def _rmsnorm(x, g):
    x32 = x.astype(F32)
    y = x32 * lax.rsqrt(jnp.mean(x32 * x32, axis=-1, keepdims=True) + EPS)
    return (y * g.astype(F32)).astype(x.dtype)


def _rope_tables(seq, dim):
    inv_freq = jnp.exp(jnp.arange(0, dim, 2, dtype=F32) * (-math.log(ROPE_THETA) / dim))
    ang = jnp.arange(seq, dtype=F32)[:, None] * inv_freq[None, :]
    return jnp.cos(ang), jnp.sin(ang)


def _apply_rope(x, cos, sin):
    x1, x2 = jnp.split(x.astype(F32), 2, axis=-1)
    return jnp.concatenate([x1 * cos - x2 * sin, x2 * cos + x1 * sin], axis=-1).astype(x.dtype)


def _heads(t, nh):
    bsz, s = t.shape[:2]
    return t.reshape(bsz, s, nh, -1).transpose(0, 2, 1, 3)


def _merge_heads(o):
    bsz, nh, s, dh = o.shape
    return o.transpose(0, 2, 1, 3).reshape(bsz, s, nh * dh)


def _pad_axis(t, size, axis):
    pad = [(0, 0)] * t.ndim
    pad[axis] = (0, size - t.shape[axis])
    return jnp.pad(t, pad)


def _moba_attention(q, k, v):
    bsz, nh, s, dh = q.shape
    nb = -(-s // MOBA_BLOCK)
    sp = nb * MOBA_BLOCK
    q, k, v = (_pad_axis(t, sp, 2) for t in (q, k, v))
    k_blocks = k.reshape(bsz, nh, nb, MOBA_BLOCK, dh)
    v_blocks = v.reshape(bsz, nh, nb, MOBA_BLOCK, dh)
    k_mean = jnp.mean(k_blocks.astype(F32), axis=3)
    topk = min(MOBA_TOPK, nb)
    scale = dh ** -0.5
    b_idx = jnp.arange(bsz)[:, None, None, None]
    h_idx = jnp.arange(nh)[None, :, None, None]
    q_off = jnp.arange(MOBA_QCHUNK)
    j_off = jnp.arange(MOBA_BLOCK)
    n_ids = jnp.arange(nb)

    def chunk(c):
        t0 = c * MOBA_QCHUNK
        own = t0 // MOBA_BLOCK
        qc = lax.dynamic_slice_in_dim(q, t0, MOBA_QCHUNK, axis=2).astype(F32)
        gate = jnp.einsum('bhqd,bhnd->bhqn', qc, k_mean)
        gate = jnp.where(n_ids < own, gate, -jnp.inf)
        gval, gidx = lax.top_k(gate, topk)
        sel_ok = jnp.isfinite(gval)
        k_sel = k_blocks[b_idx, h_idx, gidx].astype(F32)
        v_sel = v_blocks[b_idx, h_idx, gidx].astype(F32)
        s_sel = jnp.einsum('bhqd,bhqnjd->bhqnj', qc, k_sel) * scale
        s_sel = jnp.where(sel_ok[..., None], s_sel, -jnp.inf)
        s_sel = s_sel.reshape(bsz, nh, MOBA_QCHUNK, topk * MOBA_BLOCK)
        k_own = lax.dynamic_slice_in_dim(k, own * MOBA_BLOCK, MOBA_BLOCK, axis=2).astype(F32)
        v_own = lax.dynamic_slice_in_dim(v, own * MOBA_BLOCK, MOBA_BLOCK, axis=2).astype(F32)
        s_own = jnp.einsum('bhqd,bhjd->bhqj', qc, k_own) * scale
        causal = (own * MOBA_BLOCK + j_off)[None, :] <= (t0 + q_off)[:, None]
        s_own = jnp.where(causal, s_own, -jnp.inf)
        p = jax.nn.softmax(jnp.concatenate([s_sel, s_own], axis=-1), axis=-1)
        p_sel = p[..., :topk * MOBA_BLOCK].reshape(bsz, nh, MOBA_QCHUNK, topk, MOBA_BLOCK)
        o = (jnp.einsum('bhqnj,bhqnjd->bhqd', p_sel, v_sel)
             + jnp.einsum('bhqj,bhjd->bhqd', p[..., topk * MOBA_BLOCK:], v_own))
        return o.astype(q.dtype)

    out = lax.map(chunk, jnp.arange(sp // MOBA_QCHUNK))
    out = jnp.moveaxis(out, 0, 2).reshape(bsz, nh, sp, dh)
    return out[:, :, :s]


def _dilated_group(q, k, v, window, dilation):
    bsz, nh, sp, dh = q.shape
    span = window // dilation
    length = sp // dilation
    nb = length // B_QBLOCK
    scale = dh ** -0.5

    def to_blocks(t):
        t = t.astype(F32).reshape(bsz, nh, length, dilation, dh).transpose(0, 1, 3, 2, 4)
        return t.reshape(bsz, nh, dilation, nb, B_QBLOCK, dh)

    def with_prev(t):
        prev = jnp.pad(t[:, :, :, :-1], ((0, 0), (0, 0), (0, 0), (1, 0), (0, 0), (0, 0)))
        return jnp.concatenate([prev, t], axis=4)

    qb = to_blocks(q)
    kk = with_prev(to_blocks(k))
    vv = with_prev(to_blocks(v))
    sc = jnp.einsum('bhrnqd,bhrnjd->bhrnqj', qb, kk) * scale
    i = jnp.arange(B_QBLOCK)[:, None]
    j = jnp.arange(2 * B_QBLOCK)[None, :]
    dist = i + B_QBLOCK - j
    blk = jnp.arange(nb)[:, None, None]
    ok = (dist >= 0) & (dist <= span) & ((blk > 0) | (j >= B_QBLOCK))
    sc = jnp.where(ok, sc, -jnp.inf)
    m = jnp.max(sc, axis=-1, keepdims=True)
    e = jnp.exp(sc - m)
    den = jnp.sum(e, axis=-1, keepdims=True)
    o = jnp.einsum('bhrnqj,bhrnjd->bhrnqd', e, vv) / den
    lse = (m + jnp.log(den))[..., 0]
    o = o.reshape(bsz, nh, dilation, length, dh).transpose(0, 1, 3, 2, 4).reshape(bsz, nh, sp, dh)
    lse = lse.reshape(bsz, nh, dilation, length).transpose(0, 1, 3, 2).reshape(bsz, nh, sp)
    return o, lse


def _dilated_attention(q, k, v):
    s = q.shape[2]
    sp = -(-s // B_PAD) * B_PAD
    q, k, v = (_pad_axis(t, sp, 2) for t in (q, k, v))
    outs, lses = [], []
    for g, (window, dilation) in enumerate(B_GROUPS):
        sl = slice(g * B_HEADS, (g + 1) * B_HEADS)
        o, lse = _dilated_group(q[:, sl], k[:, sl], v[:, sl], window, dilation)
        outs.append(o)
        lses.append(lse)
    w = jax.nn.softmax(jnp.stack(lses), axis=0)
    o = jnp.sum(w[..., None] * jnp.stack(outs), axis=0)
    return o[:, :, :s].astype(q.dtype)


def _mla_attention(c_q, c_kv, k_rope, g_cq, g_ckv, w_uq, w_ukv, cos_r, sin_r):
    bsz, s, _ = c_q.shape
    q = _heads(_rmsnorm(c_q, g_cq) @ w_uq, C_HEADS)
    q_nope = q[..., :NOPE_DIM].astype(F32)
    q_rope = _apply_rope(q[..., NOPE_DIM:], cos_r, sin_r).astype(F32)
    kv = _heads(_rmsnorm(c_kv, g_ckv) @ w_ukv, C_HEADS)
    k_nope = kv[..., :NOPE_DIM].astype(F32)
    v = kv[..., NOPE_DIM:].astype(F32)
    k_r = _apply_rope(k_rope, cos_r, sin_r).astype(F32)
    scale = (NOPE_DIM + ROPE_DIM) ** -0.5
    kpos = jnp.arange(s)
    qoff = jnp.arange(C_QBLOCK)

    def qblock(c):
        t0 = c * C_QBLOCK
        qn = lax.dynamic_slice_in_dim(q_nope, t0, C_QBLOCK, axis=2)
        qr = lax.dynamic_slice_in_dim(q_rope, t0, C_QBLOCK, axis=2)
        sc = (jnp.einsum('bhqd,bhkd->bhqk', qn, k_nope)
              + jnp.einsum('bhqd,bkd->bhqk', qr, k_r)) * scale
        causal = kpos[None, :] <= (t0 + qoff)[:, None]
        p = jax.nn.softmax(jnp.where(causal, sc, -jnp.inf), axis=-1)
        return jnp.einsum('bhqk,bhkd->bhqd', p, v)

    out = lax.map(qblock, jnp.arange(s // C_QBLOCK))
    out = jnp.moveaxis(out, 0, 2).reshape(bsz, C_HEADS, s, V_DIM)
    return _merge_heads(out).astype(c_q.dtype)


def _hybrid_mixer(h, w_in, g_cq, g_ckv, w_uq, w_ukv, w_pa, w_pb, w_pc, w_o, cos_h, sin_h, cos_r, sin_r):
    bsz, s, _ = h.shape
    z = h @ w_in
    qa, ka, va, qb, kb, vb, cq, ckv, kr, gates = jnp.split(z, IN_OFFSETS, axis=-1)
    out_a = _moba_attention(_apply_rope(_heads(qa, A_HEADS), cos_h, sin_h),
                            _apply_rope(_heads(ka, A_HEADS), cos_h, sin_h),
                            _heads(va, A_HEADS))
    out_a = _merge_heads(out_a).astype(h.dtype)
    nbh = len(B_GROUPS) * B_HEADS
    out_b = _dilated_attention(_apply_rope(_heads(qb, nbh), cos_h, sin_h),
                               _apply_rope(_heads(kb, nbh), cos_h, sin_h),
                               _heads(vb, nbh))
    out_b = _merge_heads(out_b).astype(h.dtype)
    out_c = _mla_attention(cq, ckv, kr, g_cq, g_ckv, w_uq, w_ukv, cos_r, sin_r)
    g = jax.nn.sigmoid(gates.reshape(bsz, s, N_BRANCH, D_MODEL))
    merged = (g[:, :, 0] * (out_a @ w_pa) + g[:, :, 1] * (out_b @ w_pb)
              + g[:, :, 2] * (out_c @ w_pc))
    return merged @ w_o


def _memory_attention(h, mem, g_memkv, w_xq, w_xk, w_xv, w_xo):
    memn = _rmsnorm(mem, g_memkv)
    q = _heads(h @ w_xq, X_HEADS).astype(F32)
    k = _heads(memn @ w_xk, X_HEADS).astype(F32)
    v = _heads(memn @ w_xv, X_HEADS).astype(F32)
    p = jax.nn.softmax(jnp.einsum('bhqd,bhkd->bhqk', q, k) * HEAD_DIM ** -0.5, axis=-1)
    o = jnp.einsum('bhqk,bhkd->bhqd', p, v)
    return _merge_heads(o).astype(h.dtype) @ w_xo


def _conv_ffn(h, w_up, conv_w, conv_b, w_down):
    u = h @ w_up
    s = u.shape[1]
    up = jnp.pad(u, ((0, 0), (CONV_W - 1, 0), (0, 0)))
    c = conv_b
    for tap in range(CONV_W):
        c = c + conv_w[tap] * up[:, tap:tap + s]
    gate, val = jnp.split(c, 2, axis=-1)
    return (jax.nn.silu(gate) * val) @ w_down


def setup_inputs(seed: int = 0) -> dict:
    key = jax.random.key(seed)
    ks = jax.random.split(key, 24)

    def dense(k, shape):
        return jax.random.normal(k, shape, F32) * shape[-2] ** -0.5

    def gain(k, shape):
        return 1.0 + 0.02 * jax.random.normal(k, shape, F32)

    L = DEPTH
    return {
        'x': jax.random.normal(ks[0], (BATCH, SEQ, D_MODEL), F32),
        'mem': jax.random.normal(ks[1], (BATCH, MEM_LEN, D_MODEL), F32),
        'g_mix': gain(ks[2], (L, D_MODEL)),
        'w_in': dense(ks[3], (L, D_MODEL, D_IN)),
        'g_cq': gain(ks[4], (L, Q_LORA)),
        'g_ckv': gain(ks[5], (L, KV_LORA)),
        'w_uq': dense(ks[6], (L, Q_LORA, C_HEADS * (NOPE_DIM + ROPE_DIM))),
        'w_ukv': dense(ks[7], (L, KV_LORA, C_HEADS * (NOPE_DIM + V_DIM))),
        'w_pa': dense(ks[8], (L, A_W, D_MODEL)),
        'w_pb': dense(ks[9], (L, B_W, D_MODEL)),
        'w_pc': dense(ks[10], (L, C_W, D_MODEL)),
        'w_o': dense(ks[11], (L, D_MODEL, D_MODEL)),
        'g_mem': gain(ks[12], (L, D_MODEL)),
        'g_memkv': gain(ks[13], (L, D_MODEL)),
        'w_xq': dense(ks[14], (L, D_MODEL, X_W)),
        'w_xk': dense(ks[15], (L, D_MODEL, X_W)),
        'w_xv': dense(ks[16], (L, D_MODEL, X_W)),
        'w_xo': dense(ks[17], (L, X_W, D_MODEL)),
        'g_ffn': gain(ks[18], (L, D_MODEL)),
        'w_up': dense(ks[19], (L, D_MODEL, 2 * D_FF)),
        'conv_w': jax.random.normal(ks[20], (L, CONV_W, 2 * D_FF), F32) * CONV_W ** -0.5,
        'conv_b': 0.01 * jax.random.normal(ks[21], (L, 2 * D_FF), F32),
        'w_down': dense(ks[22], (L, D_FF, D_MODEL)),
        'g_final': gain(ks[23], (D_MODEL,)),
    }


def reference(x, mem, g_mix, w_in, g_cq, g_ckv, w_uq, w_ukv, w_pa, w_pb, w_pc, w_o,
              g_mem, g_memkv, w_xq, w_xk, w_xv, w_xo, g_ffn, w_up, conv_w, conv_b,
              w_down, g_final):
    s = x.shape[1]
    cos_h, sin_h = _rope_tables(s, HEAD_DIM)
    cos_r, sin_r = _rope_tables(s, ROPE_DIM)
    for l in range(DEPTH):
        h = _rmsnorm(x, g_mix[l])
        x = x + _hybrid_mixer(h, w_in[l], g_cq[l], g_ckv[l], w_uq[l], w_ukv[l], w_pa[l],
                              w_pb[l], w_pc[l], w_o[l], cos_h, sin_h, cos_r, sin_r)
        h = _rmsnorm(x, g_mem[l])
        x = x + _memory_attention(h, mem, g_memkv[l], w_xq[l], w_xk[l], w_xv[l], w_xo[l])
        h = _rmsnorm(x, g_ffn[l])
        x = x + _conv_ffn(h, w_up[l], conv_w[l], conv_b[l], w_down[l])
    return _rmsnorm(x, g_final)
```

```python
import math


import contextlib
import numpy as np
import concourse.bass as bass
import concourse.mybir as mybir
from concourse.bass_utils import run_bass_kernel_spmd

F32 = mybir.dt.float32
BF16 = mybir.dt.bfloat16
ALU = mybir.AluOpType
AF = mybir.ActivationFunctionType
AX = mybir.AxisListType


class Buf:
    __slots__ = ("name", "writers", "readers", "war", "excl", "base")

    def __init__(self, name, excl=False):
        self.name = name
        self.base = None
        self.excl = excl
        self.writers = []
        self.readers = []
        self.war = []


class Op:
    __slots__ = ("eng", "fn", "deps", "sig", "idx", "is_dma", "dsem")

    def __init__(self, eng, fn, is_dma=False):
        self.eng = eng
        self.fn = fn
        self.deps = []
        self.sig = False
        self.idx = None
        self.is_dma = is_dma
        self.dsem = None


class DmaSem:
    def __init__(self, name):
        self.name = name
        self.handle = None
        self.count = 0


ENGS = ("pe", "act", "dve", "pool", "sp")


class Prog:
    def __init__(self):
        self.ops = {e: [] for e in ENGS}
        self.dsems = []

    def dsem(self, name):
        d = DmaSem(name)
        self.dsems.append(d)
        return d

    def op(self, eng, fn, reads=(), writes=(), pwrites=(), dsem=None):
        o = Op(eng, fn, is_dma=dsem is not None)
        o.dsem = dsem
        deps = []
        rmw = [b for b in pwrites if any(b is r for r in reads)]
        for b in reads:
            deps.extend(b.writers)
            if b.excl:
                deps.extend(r for r in b.readers if r.eng != eng)
        for b in writes:
            deps.extend(b.readers)
            deps.extend(b.writers)
            deps.extend(b.war)
        for b in pwrites:
            deps.extend(b.readers)
            deps.extend(b.war)
            if b.base is not None:
                deps.append(b.base)
        seen = set()
        for d in deps:
            if d is o or id(d) in seen:
                continue
            if d.eng == "pe" and eng == "pe" and not d.is_dma:
                continue
            seen.add(id(d))
            o.deps.append((d, d.dsem.count if d.is_dma else None))
            d.sig = True
        if dsem is not None:
            dsem.count += 16
            o.idx = dsem.count
        for b in reads:
            if not any(b is r for r in rmw):
                b.readers.append(o)
        for b in writes:
            b.war = []
            b.readers = []
            b.writers = [o]
            b.base = o
        for b in pwrites:
            if any(b is r for r in rmw):
                b.war = b.war + b.readers
                b.readers = []
                b.writers.append(o)
            elif b.readers:
                b.war = b.war + b.readers
                b.readers = []
                b.writers = [o]
                b.base = None
            else:
                b.writers.append(o)
        self.ops[eng].append(o)
        return o

    def emit(self, nc, stack):
        esem = {}
        for e in ENGS:
            n = 0
            for o in self.ops[e]:
                if o.is_dma:
                    pass
                elif o.sig:
                    n += 1
                    o.idx = n
            esem[e] = stack.enter_context(nc.semaphore("s_" + e))
        for d in self.dsems:
            if d.count:
                d.handle = stack.enter_context(nc.semaphore("d_" + d.name))
        block = stack.enter_context(nc.Block())
        ops = self.ops

        def replay(ename, eng):
            known = {}
            for o in ops[ename]:
                need = {}
                for d, dv in o.deps:
                    if d.is_dma:
                        key, val = d.dsem.handle, dv
                    else:
                        key, val = esem[d.eng], d.idx
                    if val > known.get(key, 0) and val > need.get(key, 0):
                        need[key] = val
                for key, val in need.items():
                    eng.wait_ge(key, val)
                    known[key] = val
                ins = o.fn(eng)
                if ins is None:
                    continue
                if o.is_dma:
                    ins.then_inc(o.dsem.handle, 16)
                elif o.sig:
                    ins.then_inc(esem[ename], 1)

        @block.tensor
        def _(eng):
            replay("pe", eng)

        @block.scalar
        def _(eng):
            replay("act", eng)

        @block.vector
        def _(eng):
            replay("dve", eng)

        @block.gpsimd
        def _(eng):
            replay("pool", eng)

        @block.sync
        def _(eng):
            replay("sp", eng)
            for d in self.dsems:
                if d.count:
                    eng.wait_ge(d.handle, d.count)


class KB:
    def __init__(self):
        self.nc = bass.Bass("TRN2", target_bir_lowering=False)
        self.P = Prog()
        self.st = contextlib.ExitStack()
        self.bufs = {}
        self.nps = 0
        self.outs = []

    def din(self, name, shape, dt):
        return self.nc.dram_tensor(name, list(shape), dt, kind="ExternalInput").ap()

    def dout(self, name, shape, dt):
        ap = self.nc.dram_tensor(name, list(shape), dt, kind="ExternalOutput").ap()
        b = Buf(name)
        self.outs.append(b)
        return ap, b

    def sb(self, name, shape, dt):
        t = self.st.enter_context(self.nc.sbuf_tensor(name, list(shape), dt))
        return t, Buf(name)

    def ps(self, name, shape, dt):
        t = self.st.enter_context(self.nc.psum_tensor(name, list(shape), dt))
        return t, Buf(name, excl=True)

    def slots(self, name, n, shape, dt, dma=False):
        res = []
        for i in range(n):
            t, b = self.sb(f"{name}{i}", shape, dt)
            res.append((t, b, self.P.dsem(f"{name}{i}") if dma else None))
        return res

    def load(self, out_ap, in_ap, buf, dsem, eng="sp", reads=()):
        return self.P.op(eng, lambda e: e.dma_start(out=out_ap, in_=in_ap), reads=list(reads), writes=[buf], dsem=dsem)

    def store(self, out_ap, in_ap, src_buf, obuf, dsem, eng="sp"):
        return self.P.op(eng, lambda e: e.dma_start(out=out_ap, in_=in_ap), reads=[src_buf], pwrites=[obuf], dsem=dsem)

    def finish(self):
        self.P.op("sp", lambda e: None, reads=self.outs)
        self.P.emit(self.nc, self.st)
        self.st.close()
        return self.nc


class Rot:
    def __init__(self, items):
        self.items = items
        self.i = 0

    def next(self):
        it = self.items[self.i % len(self.items)]
        self.i += 1
        return it


def run_spmd(nc, in_maps):
    res = run_bass_kernel_spmd(nc, in_maps, core_ids=list(range(len(in_maps))))
    return res.results


import ml_dtypes
NPBF = ml_dtypes.bfloat16


def rope_tables(pos):
    pos = pos.astype(np.float32)
    invH = np.exp(np.arange(0, 128, 2, dtype=np.float32) * np.float32(-math.log(10000.0) / 128)).astype(np.float32)
    invR = np.exp(np.arange(0, 64, 2, dtype=np.float32) * np.float32(-math.log(10000.0) / 64)).astype(np.float32)
    angH = (pos[None, :] * invH[:, None]).astype(np.float32)
    angR = (pos[None, :] * invR[:, None]).astype(np.float32)
    cH = np.cos(angH.astype(np.float64)).astype(np.float32)
    sH = np.sin(angH.astype(np.float64)).astype(np.float32)
    cR = np.cos(angR.astype(np.float64)).astype(np.float32)
    sR = np.sin(angR.astype(np.float64)).astype(np.float32)
    cosH = np.concatenate([cH, cH], 0)
    sinH = np.concatenate([-sH, sH], 0)
    cosR = np.concatenate([cR, cR, cR, cR], 0)
    sinR = np.concatenate([-sR, sR, -sR, sR], 0)
    return [np.ascontiguousarray(a) for a in (cosH, sinH, cosR, sinR)]


def perm_consts():
    m = np.arange(128)
    p128 = np.zeros((128, 128), np.float32)
    p128[(m + 64) % 128, m] = 1
    p64 = np.zeros((128, 128), np.float32)
    p64[(m // 64) * 64 + ((m % 64) + 32) % 64, m] = 1
    return p128.astype(NPBF), p64.astype(NPBF), np.eye(128, dtype=np.float32).astype(NPBF), np.ones((128, 128), np.float32).astype(NPBF)


def gcol(g, n):
    return np.ascontiguousarray(g.reshape(n, 128).T)


def gbcast(g):
    return np.ascontiguousarray(np.broadcast_to(g.reshape(16, 128).T[:, :, None], (128, 16, 128)))


def prep_A_weights(wb, l):
    w_in = wb["w_in"][l]
    o = [0, 512, 1024, 1536, 3072, 4608, 6144, 7680, 8192, 8256, 14400]
    qa, ka, va, qb, kb_, vb, cq, ckv, kr, gates = [w_in[:, o[i]:o[i + 1]] for i in range(10)]
    w_uq = wb["w_uq"][l].reshape(1536, 8, 192)
    w_ukv = wb["w_ukv"][l].reshape(512, 8, 256)
    return {
        "w_qk": np.ascontiguousarray(np.concatenate([qa, ka, qb, kb_], 1)),
        "w_v": np.ascontiguousarray(np.concatenate([va, vb], 1)),
        "w_lat": np.ascontiguousarray(np.concatenate([cq, ckv, kr], 1)),
        "w_uqn": np.ascontiguousarray(w_uq[:, :, :128].reshape(1536, 1024)),
        "w_uqr": np.ascontiguousarray(w_uq[:, :, 128:].reshape(1536, 512)),
        "w_ukn": np.ascontiguousarray(w_ukv[:, :, :128].reshape(512, 1024)),
        "w_ukvv": np.ascontiguousarray(w_ukv[:, :, 128:].reshape(512, 1024)),
        "w_gates": np.ascontiguousarray(gates),
    }


CAST_F = 2048


def build_cast(ntiles):
    kb = KB()
    src = kb.din("src", [ntiles, 128, CAST_F], F32)
    dst, dstb = kb.dout("dst", [ntiles, 128, CAST_F], BF16)
    ins = Rot(kb.slots("ci", 3, [128, CAST_F], F32, dma=True))
    outs = Rot(kb.slots("co", 3, [128, CAST_F], BF16, dma=True))
    engs = ["dve", "act", "pool"]
    for t in range(ntiles):
        it, ib, isem = ins.next()
        ot, ob, osem = outs.next()
        kb.load(it[:], src[t], ib, isem)
        e = engs[t % 3]
        if e == "act":
            kb.P.op("act", lambda e_, ot=ot, it=it: e_.copy(out=ot[:], in_=it[:]), reads=[ib], writes=[ob])
        else:
            kb.P.op(e, lambda e_, ot=ot, it=it: e_.tensor_copy(out=ot[:], in_=it[:]), reads=[ib], writes=[ob])
        kb.store(dst[t], ot[:], ob, dstb, osem, eng="sp")
    return kb.finish()


def cast_weights(arrs):
    flat = np.concatenate([a.reshape(-1) for a in arrs])
    per = 8 * 128 * CAST_F
    n = -(-flat.size // per) * per
    pad = np.zeros(n, np.float32)
    pad[:flat.size] = flat
    nt = n // per
    sh = pad.reshape(8, nt, 128, CAST_F)
    nc = build_cast(nt)
    res = run_spmd(nc, [{"src": sh[c]} for c in range(8)])
    out = np.concatenate([r["dst"].reshape(-1) for r in res])
    outs = []
    o = 0
    for a in arrs:
        outs.append(out[o:o + a.size].reshape(a.shape))
        o += a.size
    return outs


EPS = 1e-6


def rstd_ops(P, out_ap, out_buf, in_ap, in_buf, n, eps_ap, eps_buf, partial=False):
    kw = {"pwrites": [out_buf]} if partial else {"writes": [out_buf]}
    P.op("act", lambda e: e.activation(out=out_ap, in_=in_ap, func=AF.Sqrt, bias=eps_ap, scale=1.0 / n),
         reads=[in_buf, eps_buf], **kw)
    P.op("dve", lambda e: e.reciprocal(out=out_ap, in_=out_ap), reads=[out_buf], **kw)


def norm_transpose(kb, x_ap, rows, hT, hT_buf, col0, gb, gb_buf, ident, ident_buf, xs_rot, ps_tr, partial, eps_ap, eps_buf):
    P = kb.P
    (xt, xb, xsem), (xs, xsb), (sq, sqb), (ss, ssb), (rs, rsb) = xs_rot.next()
    kb.load(xt[0:rows, :], x_ap, xb, xsem)
    P.op("act", lambda e: e.activation(out=sq[0:rows, :], in_=xt[0:rows, :], func=AF.Square, accum_out=ss[0:rows, :]),
         reads=[xb], writes=[sqb, ssb])
    rstd_ops(P, rs[0:rows, :], rsb, ss[0:rows, :], ssb, 2048.0, eps_ap[0:rows, :], eps_buf)
    P.op("act", lambda e: e.activation(out=xs[0:rows, :], in_=xt[0:rows, :], func=AF.Copy, scale=rs[0:rows, 0:1]),
         reads=[xb, rsb], writes=[xsb])
    for half in range(2):
        pt, ptb = ps_tr.next()
        for k in range(8):
            kc = half * 8 + k
            P.op("pe", lambda e, k=k, kc=kc, pt=pt: e.transpose(out=pt[:, k, 0:rows], in_=xs[0:rows, kc * 128:(kc + 1) * 128],
                                                               identity=ident[0:rows, 0:rows]),
                 reads=[xsb, ident_buf], pwrites=[ptb])
        P.op("dve", lambda e, half=half, pt=pt: e.tensor_tensor(out=hT[:, half * 8:(half + 1) * 8, col0:col0 + rows],
                                                                in0=pt[:, :, 0:rows], in1=gb[:, half * 8:(half + 1) * 8, 0:rows],
                                                                op=ALU.mult),
             reads=[ptb, gb_buf], **({"pwrites": [hT_buf]} if partial else {"writes": [hT_buf]}))
        partial = True


def make_xs_rot(kb, n=2):
    items = []
    for i in range(n):
        xt, xb = kb.sb(f"xt{i}", [128, 2048], F32)
        xsem = kb.P.dsem(f"xt{i}")
        items.append(((xt, xb, xsem), kb.sb(f"xs{i}", [128, 2048], BF16), kb.sb(f"xsq{i}", [128, 2048], BF16),
                      kb.sb(f"xss{i}", [128, 1], F32), kb.sb(f"xrs{i}", [128, 1], F32)))
    return Rot(items)


def load_const(kb, name, ap, shape, dt):
    t, b = kb.sb(name, shape, dt)
    kb.load(t[:], ap, b, kb.P.dsem(name))
    return t, b


def wload(kb, wrot, w_ap, K, c0, nc_):
    wt, wb, wsem = wrot.next()
    KC = K // 128
    src = w_ap.rearrange("(kc p) n -> p kc n", p=128)
    for k0 in range(0, KC, 4):
        kb.P.op("sp", lambda e, k0=k0: e.dma_start(out=wt[:, k0:k0 + 4, 0:nc_], in_=src[:, k0:k0 + 4, c0:c0 + nc_]),
                dsem=wsem, **({"writes": [wb]} if k0 == 0 else {"pwrites": [wb]}))
    return wt, wb


def build_A(TOK=4096, TB=1024, stop=99, dbg=99):
    kb = KB()
    P = kb.P
    NTG = TB // 512
    x = kb.din("x", [TOK, 2048], F32)
    gmix = kb.din("gmix_b", [128, 16, 128], F32)
    w_qk = kb.din("w_qk", [2048, 4096], BF16)
    w_v = kb.din("w_v", [2048, 2048], BF16)
    w_lat = kb.din("w_lat", [2048, 2112], BF16)
    gcq = kb.din("gcq", [128, 12], F32)
    gckv = kb.din("gckv", [128, 4], F32)
    w_uqn = kb.din("w_uqn", [1536, 1024], BF16)
    w_uqr = kb.din("w_uqr", [1536, 512], BF16)
    w_ukn = kb.din("w_ukn", [512, 1024], BF16)
    w_ukvv = kb.din("w_ukvv", [512, 1024], BF16)
    cosH = kb.din("cosH", [128, TOK], F32)
    sinH = kb.din("sinH", [128, TOK], F32)
    cosR = kb.din("cosR", [128, TOK], F32)
    sinR = kb.din("sinR", [128, TOK], F32)
    p128_d = kb.din("p128", [128, 128], BF16)
    p64_d = kb.din("p64", [128, 128], BF16)
    ident_d = kb.din("ident", [128, 128], BF16)
    ones_d = kb.din("ones", [128, 128], BF16)

    qkT, qkTb = kb.dout("qkT", [32, 128, TOK], BF16)
    vab, vabb = kb.dout("vab", [TOK, 2048], BF16)
    kmean, kmeanb = kb.dout("kmean", [128, 4, TOK // 256], F32)
    qnT, qnTb = kb.dout("qnT", [8, 128, TOK], BF16)
    qrT, qrTb = kb.dout("qrT", [4, 128, TOK], BF16)
    knT, knTb = kb.dout("knT", [8, 128, TOK], BF16)
    krT, krTb = kb.dout("krT", [64, TOK], BF16)
    vc, vcb = kb.dout("vc", [TOK, 1024], BF16)

    gb, gbb = load_const(kb, "gb", gmix, [128, 16, 128], F32)
    gq, gqb = load_const(kb, "gq", gcq, [128, 12], F32)
    gk, gkb = load_const(kb, "gk", gckv, [128, 4], F32)
    p128, p128b = load_const(kb, "p128s", p128_d, [128, 128], BF16)
    p64, p64b = load_const(kb, "p64s", p64_d, [128, 128], BF16)
    ident, identb = load_const(kb, "idents", ident_d, [128, 128], BF16)
    ones, onesb = load_const(kb, "oness", ones_d, [128, 128], BF16)

    epst, epsb = kb.sb("epst", [128, 1], F32)
    P.op("dve", lambda e: e.memset(epst[:], EPS), writes=[epsb])
    hT, hTb = kb.sb("hT", [128, 16, TB], BF16)
    cqT, cqTb = kb.sb("cqT", [128, 12, TB], BF16)
    ckT, ckTb = kb.sb("ckT", [128, 4, TB], BF16)
    sqk, sqkb = kb.sb("sqk", [128, 4, TB], BF16)
    rq, rqb = kb.sb("rq", [128, TB], F32)
    rk, rkb = kb.sb("rk", [128, TB], F32)
    rkc, rkcb = kb.sb("rkc", [128, TB // 128], F32)
    tabs = [kb.sb(n, [128, TB], F32) for n in ("cH", "sH", "cR", "sR")]
    tsem = [P.dsem(n) for n in ("cH", "sH", "cR", "sR")]
    km, kmb = kb.sb("km", [128, 4, TOK // 256], F32)

    xs_rot = make_xs_rot(kb)
    wrot = Rot(kb.slots("w", 3, [128, 16, 512], BF16, dma=True))
    ps_tr = Rot([kb.ps(f"ptr{i}", [128, 8, 128], BF16) for i in range(2)])
    ps_mm = Rot([kb.ps(f"pmm{i}", [128, 512], F32) for i in range(3)])
    ps_sw = Rot([kb.ps(f"psw{i}", [128, 512], F32) for i in range(2)])
    ps_ss, ps_ssb = kb.ps("pss", [128, 512], F32)
    raw_rot = Rot([kb.sb(f"raw{i}", [128, 512], BF16) for i in range(2)])
    t1_rot = Rot([kb.sb(f"t1_{i}", [128, 512], F32) for i in range(2)])
    t2_rot = Rot([kb.sb(f"t2_{i}", [128, 512], F32) for i in range(2)])
    sq_rot = Rot([kb.sb(f"sqt{i}", [128, 512], BF16) for i in range(2)])
    ost = Rot(kb.slots("ost", 4, [128, 512], BF16, dma=True))

    def rope_epi(pm, pmb, tg, cos_i, sin_i, perm, permb, out_ap, obuf, pre_scale=None, ksum=None, rows=128):
        cs, csb = tabs[cos_i]
        sn, snb = tabs[sin_i]
        sl = slice(tg * 512, (tg + 1) * 512)
        raw, rawb = raw_rot.next()
        t1, t1b = t1_rot.next()
        t2, t2b = t2_rot.next()
        psw, pswb = ps_sw.next()
        if pre_scale is None:
            P.op("act", lambda e: e.copy(out=raw[0:rows, :], in_=pm[0:rows, :]), reads=[pmb], writes=[rawb])
            P.op("dve", lambda e: e.tensor_tensor(out=t1[0:rows, :], in0=pm[0:rows, :], in1=cs[0:rows, sl], op=ALU.mult),
                 reads=[pmb, csb], writes=[t1b])
        else:
            rt, rtb = pre_scale
            P.op("dve", lambda e: e.tensor_tensor(out=t2[0:rows, :], in0=pm[0:rows, :], in1=rt[0:rows, sl], op=ALU.mult),
                 reads=[pmb, rtb], writes=[t2b])
            P.op("act", lambda e: e.copy(out=raw[0:rows, :], in_=t2[0:rows, :]), reads=[t2b], writes=[rawb])
            P.op("dve", lambda e: e.tensor_tensor(out=t1[0:rows, :], in0=t2[0:rows, :], in1=cs[0:rows, sl], op=ALU.mult),
                 reads=[t2b, csb], writes=[t1b])
        P.op("pe", lambda e: e.matmul(psw[0:rows, :], lhsT=perm[0:rows, 0:rows], rhs=raw[0:rows, :], start=True, stop=True),
             reads=[rawb, permb], writes=[pswb])
        P.op("dve", lambda e: e.tensor_tensor(out=t2[0:rows, :], in0=psw[0:rows, :], in1=sn[0:rows, sl], op=ALU.mult),
             reads=[pswb, snb], writes=[t2b])
        P.op("pool", lambda e: e.tensor_tensor(out=t1[0:rows, :], in0=t1[0:rows, :], in1=t2[0:rows, :], op=ALU.add),
             reads=[t1b, t2b], writes=[t1b])
        ot, ob, osem = ost.next()
        P.op("act", lambda e: e.copy(out=ot[0:rows, :], in_=t1[0:rows, :]), reads=[t1b], writes=[ob])
        if ksum is not None:
            P.op("dve", lambda e: e.tensor_reduce(out=ksum, in_=t1[:, :].rearrange("p (b t) -> p b t", t=256),
                                                  axis=AX.X, op=ALU.add), reads=[t1b], pwrites=[kmb])
        kb.store(out_ap, ot[0:rows, :], ob, obuf, osem)

    def scale_epi(pm, pmb, tg, rt, rtb, out_ap, obuf):
        sl = slice(tg * 512, (tg + 1) * 512)
        ot, ob, osem = ost.next()
        P.op("dve", lambda e: e.tensor_tensor(out=ot[:, :], in0=pm[:, :], in1=rt[:, sl], op=ALU.mult),
             reads=[pmb, rtb], writes=[ob])
        kb.store(out_ap, ot[:, :], ob, obuf, osem)

    def gemm_fm(w_ap, K, ncols, aT, aTb, epi):
        KC = K // 128
        for c0 in range(0, ncols, 512):
            nc_ = min(512, ncols - c0)
            wt, wb = wload(kb, wrot, w_ap, K, c0, nc_)
            for f0 in range(0, nc_, 128):
                rows = min(128, nc_ - f0)
                for tg in range(NTG):
                    pm, pmb = ps_mm.next()
                    for kc in range(KC):
                        P.op("pe", lambda e, pm=pm, wt=wt, kc=kc, f0=f0, rows=rows, tg=tg: e.matmul(
                            pm[0:rows, :], lhsT=wt[:, kc, f0:f0 + rows], rhs=aT[:, kc, tg * 512:(tg + 1) * 512],
                            start=(kc == 0), stop=(kc == KC - 1)),
                            reads=[wb, aTb], **({"writes": [pmb]} if kc == 0 else {"pwrites": [pmb]}))
                    epi((c0 + f0) // 128, tg, pm, pmb, rows)

    def gemm_tm(w_ap, K, ncols, aT, aTb, epi):
        KC = K // 128
        for c0 in range(0, ncols, 512):
            wt, wb = wload(kb, wrot, w_ap, K, c0, 512)
            for t in range(TB // 128):
                pm, pmb = ps_mm.next()
                for kc in range(KC):
                    P.op("pe", lambda e, pm=pm, wt=wt, kc=kc, t=t: e.matmul(
                        pm[:, :], lhsT=aT[:, kc, t * 128:(t + 1) * 128], rhs=wt[:, kc, 0:512],
                        start=(kc == 0), stop=(kc == KC - 1)),
                        reads=[wb, aTb], **({"writes": [pmb]} if kc == 0 else {"pwrites": [pmb]}))
                epi(c0 // 512, t, pm, pmb)

    for blk in range(TOK // TB):
        t0 = blk * TB
        for (tt, tb_), d, sem in zip(tabs, (cosH, sinH, cosR, sinR), tsem):
            kb.load(tt[:], d[:, t0:t0 + TB], tb_, sem)
        for t in range(TB // 128):
            norm_transpose(kb, x[t0 + t * 128:t0 + (t + 1) * 128, :], 128, hT, hTb, t * 128, gb, gbb, ident, identb,
                           xs_rot, ps_tr, partial=(t > 0), eps_ap=epst, eps_buf=epsb)

        if stop < 1:
            continue
        def lat_epi(ft, tg, pm, pmb, rows):
            sl = slice(tg * 512, (tg + 1) * 512)
            if dbg < 1:
                return
            if ft >= 16 and dbg < 4:
                return
            if ft < 16:
                isq = ft < 12
                sq, sqb = sq_rot.next()
                if isq:
                    P.op("act", lambda e: e.activation(out=sq[:, :], in_=pm[:, :], func=AF.Square), reads=[pmb], writes=[sqb])
                    P.op("dve", lambda e: e.tensor_scalar(out=cqT[:, ft, sl], in0=pm[:, :], scalar1=gq[:, ft:ft + 1], scalar2=None,
                                                          op0=ALU.mult), reads=[pmb, gqb], pwrites=[cqTb])
                    sq_ap, sq_b = sq[:, :], sqb
                    first, last = ft == 0, ft == 11
                else:
                    f = ft - 12
                    P.op("act", lambda e: e.activation(out=sqk[:, f, sl], in_=pm[:, :], func=AF.Square), reads=[pmb], pwrites=[sqkb])
                    P.op("dve", lambda e: e.tensor_scalar(out=ckT[:, f, sl], in0=pm[:, :], scalar1=gk[:, f:f + 1], scalar2=None,
                                                          op0=ALU.mult), reads=[pmb, gkb], pwrites=[ckTb])
                    sq_ap, sq_b = sqk[:, f, sl], sqkb
                    first, last = f == 0, f == 3
                if dbg < 2:
                    return
                P.op("pe", lambda e: e.matmul(ps_ss[:, :], lhsT=ones[:, :], rhs=sq_ap, start=first, stop=last),
                     reads=[onesb, sq_b], **({"writes": [ps_ssb]} if first else {"pwrites": [ps_ssb]}))
                if last and dbg >= 3:
                    n = 1536.0 if isq else 512.0
                    rt, rtb = (rq, rqb) if isq else (rk, rkb)
                    rstd_ops(P, rt[:, sl], rtb, ps_ss[:, :], ps_ssb, n, epst[:, :], epsb, partial=True)
            else:
                rope_epi(pm, pmb, tg, 2, 3, p64, p64b, krT[:, t0 + tg * 512:t0 + (tg + 1) * 512], krTb, rows=64)

        KC = 16
        lat_w = []
        for c0 in range(0, 2112, 512):
            nc_ = min(512, 2112 - c0)
            lat_w.append((c0, nc_))
        for tg in range(NTG):
            for c0, nc_ in lat_w:
                wt, wb = wload(kb, wrot, w_lat, 2048, c0, nc_)
                for f0 in range(0, nc_, 128):
                    rows = min(128, nc_ - f0)
                    pm, pmb = ps_mm.next()
                    for kc in range(KC):
                        P.op("pe", lambda e, pm=pm, wt=wt, kc=kc, f0=f0, rows=rows, tg=tg: e.matmul(
                            pm[0:rows, :], lhsT=wt[:, kc, f0:f0 + rows], rhs=hT[:, kc, tg * 512:(tg + 1) * 512],
                            start=(kc == 0), stop=(kc == KC - 1)),
                            reads=[wb, hTb], **({"writes": [pmb]} if kc == 0 else {"pwrites": [pmb]}))
                    lat_epi((c0 + f0) // 128, tg, pm, pmb, rows)
        if stop < 2:
            continue
        for t in range(TB // 128):
            pm, pmb = ps_mm.next()
            for f in range(4):
                P.op("pe", lambda e, pm=pm, f=f, t=t: e.matmul(pm[:, 0:1], lhsT=sqk[:, f, t * 128:(t + 1) * 128], rhs=ones[:, 0:1],
                                                               start=(f == 0), stop=(f == 3)),
                     reads=[sqkb, onesb], **({"writes": [pmb]} if f == 0 else {"pwrites": [pmb]}))
            rstd_ops(P, rkc[:, t:t + 1], rkcb, pm[:, 0:1], pmb, 512.0, epst[:, :], epsb, partial=True)

        if stop < 3:
            continue
        gemm_fm(w_uqn, 1536, 1024, cqT, cqTb,
                lambda ft, tg, pm, pmb, rows: scale_epi(pm, pmb, tg, rq, rqb, qnT[ft, :, t0 + tg * 512:t0 + (tg + 1) * 512], qnTb))
        gemm_fm(w_uqr, 1536, 512, cqT, cqTb,
                lambda ft, tg, pm, pmb, rows: rope_epi(pm, pmb, tg, 2, 3, p64, p64b,
                                                       qrT[ft, :, t0 + tg * 512:t0 + (tg + 1) * 512], qrTb, pre_scale=(rq, rqb)))
        gemm_fm(w_ukn, 512, 1024, ckT, ckTb,
                lambda ft, tg, pm, pmb, rows: scale_epi(pm, pmb, tg, rk, rkb, knT[ft, :, t0 + tg * 512:t0 + (tg + 1) * 512], knTb))

        def vc_epi(cb, t, pm, pmb):
            ot, ob, osem = ost.next()
            P.op("act", lambda e: e.activation(out=ot[:, :], in_=pm[:, :], func=AF.Copy, scale=rkc[:, t:t + 1]),
                 reads=[pmb, rkcb], writes=[ob])
            kb.store(vc[t0 + t * 128:t0 + (t + 1) * 128, cb * 512:(cb + 1) * 512], ot[:, :], ob, vcb, osem)
        gemm_tm(w_ukvv, 512, 1024, ckT, ckTb, vc_epi)

        if stop < 4:
            continue
        def qk_epi(ft, tg, pm, pmb, rows):
            ksum = None
            if 4 <= ft < 8:
                b0 = (t0 + tg * 512) // 256
                ksum = km[:, ft - 4, b0:b0 + 2]
            rope_epi(pm, pmb, tg, 0, 1, p128, p128b, qkT[ft, :, t0 + tg * 512:t0 + (tg + 1) * 512], qkTb, ksum=ksum)
        gemm_fm(w_qk, 2048, 4096, hT, hTb, qk_epi)

        if stop < 5:
            continue
        def v_epi(cb, t, pm, pmb):
            ot, ob, osem = ost.next()
            P.op("act", lambda e: e.copy(out=ot[:, :], in_=pm[:, :]), reads=[pmb], writes=[ob])
            kb.store(vab[t0 + t * 128:t0 + (t + 1) * 128, cb * 512:(cb + 1) * 512], ot[:, :], ob, vabb, osem)
        gemm_tm(w_v, 2048, 2048, hT, hTb, v_epi)

    kmo, kmob = kb.sb("kmo", [128, 4, TOK // 256], F32)
    if stop < 4:
        return kb.finish()
    P.op("act", lambda e: e.mul(out=kmo[:], in_=km[:], mul=1.0 / 256), reads=[kmb], writes=[kmob])
    kb.store(kmean[:, :, :], kmo[:], kmob, kmeanb, P.dsem("kmo"))
    return kb.finish()


NEG = -32768.0
DIL = ((128, 1), (512, 4), (2048, 16))


def dil_masks():
    k = np.arange(128)[:, None]
    q = np.arange(128)[None, :]
    ms = []
    for w, d in DIL:
        for off in range(w // 128 + 1):
            dist = 128 * off + q - k
            ok = (dist >= 0) & (dist <= w) & (dist % d == 0)
            ms.append(np.where(ok, 0.0, NEG))
    return np.ascontiguousarray(np.stack(ms, 1).astype(np.float32)).astype(NPBF)


def attn_consts(S):
    k = np.arange(128)[:, None]
    q = np.arange(128)[None, :]
    tri = np.where(k <= q, 0.0, NEG).astype(np.float32).astype(NPBF)
    nb = S // 256
    E = np.zeros((64, nb, 128), np.float32)
    for n in range(nb):
        E[n, n, :] = 1
    return {"tri": tri, "Eoh": E.astype(NPBF), "dmask": dil_masks(),
            "ident": np.eye(128, dtype=np.float32).astype(NPBF), "ones": np.ones((128, 128), np.float32).astype(NPBF)}


def build_B(S=16384, parts=("mla", "moba", "dil")):
    kb = KB()
    P = kb.P
    NQG = S // 512
    NKT = S // 128
    NB = S // 256
    qn = kb.din("qn", [2, 128, S], BF16)
    qr = kb.din("qr", [128, S], BF16)
    kn = kb.din("kn", [2, 128, S], BF16)
    kr2 = kb.din("kr2", [128, S], BF16)
    vc = kb.din("vc", [S, 256], BF16)
    qa = kb.din("qa", [128, S], BF16)
    ka = kb.din("ka", [128, S], BF16)
    va = kb.din("va", [S, 128], BF16)
    kmean = kb.din("kmean", [128, NB], F32)
    qd = kb.din("qd", [3, 128, S], BF16)
    kd = kb.din("kd", [3, 128, S], BF16)
    vd = kb.din("vd", [S, 384], BF16)
    tri_d = kb.din("tri", [128, 128], BF16)
    E_d = kb.din("Eoh", [64, NB, 128], BF16)
    dm_d = kb.din("dmask", [128, 24, 128], BF16)
    ident_d = kb.din("ident", [128, 128], BF16)
    ones_d = kb.din("ones", [128, 128], BF16)
    ocT, ocTb = kb.dout("ocT", [2, 128, S], BF16)
    oaT, oaTb = kb.dout("oaT", [128, S], BF16)
    obT, obTb = kb.dout("obT", [128, S], BF16)

    tri, trib = load_const(kb, "tri_s", tri_d, [128, 128], BF16)
    ident, identb = load_const(kb, "ident_s", ident_d, [128, 128], BF16)
    ones, onesb = load_const(kb, "ones_s", ones_d, [128, 128], BF16)
    dm, dmb = load_const(kb, "dm_s", dm_d, [128, 24, 128], BF16)
    Eoh, Eb = load_const(kb, "E_s", E_d, [64, NB, 128], BF16)

    Kbuf, Kb = kb.sb("Kbuf", [128, S], BF16)
    Ksem = P.dsem("Kbuf")
    KRbuf, KRb = kb.sb("KRbuf", [128, S], BF16)
    KRsem = P.dsem("KRbuf")
    Vbuf, Vb = kb.sb("Vbuf", [128, NKT, 128], BF16)
    Vsem = P.dsem("Vbuf")
    qrot = Rot(kb.slots("qt", 6, [128, 512], BF16, dma=True))
    prot = Rot([kb.sb(f"pt{i}", [128, 512], BF16) for i in range(4)])
    NS = 3
    ps_s = Rot([kb.ps(f"pS{i}", [128, 512], F32) for i in range(NS)])
    ps_o = Rot([kb.ps(f"pO{i}", [128, 512], F32) for i in range(2)])
    ps_d = Rot([kb.ps(f"pD{i}", [128, 512], F32) for i in range(1)])
    dacc_rot = Rot([kb.sb(f"dacc{i}", [128, 512], F32) for i in range(2)])
    onesf, onesfb = kb.sb("onesf", [128, 128], F32)
    P.op("dve", lambda e: e.memset(onesf[:], 1.0), writes=[onesfb])
    ps_x, ps_xb = kb.ps("pX", [128, 512], F32)
    ps_t, ps_tb = kb.ps("pT", [128, 8, 128], BF16)
    rden_rot = Rot([kb.sb(f"rden{i}", [128, 512], F32) for i in range(2)])
    ost = Rot(kb.slots("ost", 3, [128, 512], BF16, dma=True))

    def load_rows(dst, dbuf, dsem, src_ap, nchunk=4):
        w = S // nchunk
        for i in range(nchunk):
            P.op("sp", lambda e, i=i: e.dma_start(out=dst[:, i * w:(i + 1) * w], in_=src_ap[:, i * w:(i + 1) * w]),
                 dsem=dsem, **({"writes": [dbuf]} if i == 0 else {"pwrites": [dbuf]}))

    def load_v(src_ap, c0):
        v3 = src_ap.rearrange("(kt p) d -> p kt d", p=128)
        for k0 in range(0, NKT, 8):
            P.op("sp", lambda e, k0=k0: e.dma_start(out=Vbuf[:, k0:k0 + 8, :], in_=v3[:, k0:k0 + 8, c0:c0 + 128]),
                 dsem=Vsem, **({"writes": [Vb]} if k0 == 0 else {"pwrites": [Vb]}))

    def load_q(src_ap, j):
        qt, qb_, qsem = qrot.next()
        kb.load(qt[:], src_ap[:, j * 512:(j + 1) * 512], qb_, qsem)
        return qt, qb_

    def attn_group(blocks, scale, out_ap, obuf):
        po, pob = ps_o.next()
        da, dab = dacc_rot.next()
        P.op("dve", lambda e: e.memset(po[:], 0.0), writes=[pob])
        P.op("dve", lambda e: e.memset(da[:], 0.0), writes=[dab])

        def emit_s(bk):
            psS, psSb = ps_s.next()
            n = len(bk["s_terms"])
            for i, (lhsT, rhs, rd, cc0, cc1) in enumerate(bk["s_terms"]):
                P.op("pe", lambda e, lhsT=lhsT, rhs=rhs, i=i, cc0=cc0, cc1=cc1: e.matmul(
                    psS[:, cc0:cc1], lhsT=lhsT, rhs=rhs, start=(i == 0), stop=(i == n - 1)),
                    reads=rd, **({"writes": [psSb]} if i == 0 else {"pwrites": [psSb]}))
            return psS, psSb

        depth = NS - 1
        pend = [emit_s(bk) for bk in blocks[:depth]]
        for bi, bk in enumerate(blocks):
            psS, psSb = pend.pop(0)
            if bi + depth < len(blocks):
                pend.append(emit_s(blocks[bi + depth]))
            c0, c1 = bk["c0"], bk["c1"]
            pt, ptb = prot.next()
            P.op("act", lambda e, psS=psS, pt=pt, c0=c0, c1=c1: e.activation(out=pt[:, c0:c1], in_=psS[:, c0:c1], func=AF.Exp, scale=scale),
                 reads=[psSb], writes=[ptb])
            P.op("pe", lambda e, pt=pt, c0=c0, c1=c1, v=bk["v_ap"]: e.matmul(po[:, c0:c1], lhsT=v, rhs=pt[:, c0:c1], start=False, stop=True,
                                                                              skip_group_check=True),
                 reads=[ptb] + bk["v_reads"], pwrites=[pob])
            P.op("dve", lambda e, pt=pt, c0=c0, c1=c1: e.tensor_tensor(out=da[:, c0:c1], in0=da[:, c0:c1], in1=pt[:, c0:c1], op=ALU.add),
                 reads=[ptb, dab], pwrites=[dab])
        pd, pdb = ps_d.next()
        P.op("pe", lambda e: e.matmul(pd[:, :], lhsT=onesf[:, :], rhs=da[:, :], start=True, stop=True), reads=[onesfb, dab], writes=[pdb])
        rd_, rdb = rden_rot.next()
        ot, ob, osem = ost.next()
        P.op("dve", lambda e: e.reciprocal(out=rd_[:], in_=pd[:]), reads=[pdb], writes=[rdb])
        P.op("dve", lambda e: e.tensor_tensor(out=ot[:], in0=po[:], in1=rd_[:], op=ALU.mult), reads=[pob, rdb], writes=[ob])
        kb.store(out_ap, ot[:], ob, obuf, osem)

    def causal_blocks(j, mk_terms, vreads):
        blocks = []
        for kt in range(4 * (j + 1)):
            diag = kt >= 4 * j
            c0 = 128 * (kt - 4 * j) if diag else 0
            terms = mk_terms(kt, c0, 512)
            if diag:
                terms.append((ident[:, :], tri[:, :], [identb, trib], c0, c0 + 128))
            blocks.append({"c0": c0, "c1": 512, "s_terms": terms, "v_ap": Vbuf[:, kt, :], "v_reads": vreads})
        return blocks

    if "mla" in parts:
        load_rows(KRbuf, KRb, KRsem, kr2)
        for h in range(2):
            load_rows(Kbuf, Kb, Ksem, kn[h])
            load_v(vc, h * 128)
            hp = h * 64
            for j in range(NQG):
                qnt, qnb = load_q(qn[h], j)
                qrt, qrb = load_q(qr, j)

                def mk(kt, c0, c1, qnt=qnt, qnb=qnb, qrt=qrt, qrb=qrb):
                    return [(Kbuf[:, kt * 128:(kt + 1) * 128], qnt[:, c0:c1], [Kb, qnb], c0, c1),
                            (KRbuf[hp:hp + 64, kt * 128:(kt + 1) * 128], qrt[hp:hp + 64, c0:c1], [KRb, qrb], c0, c1)]
                attn_group(causal_blocks(j, mk, [Vb]), 192.0 ** -0.5, ocT[h, :, j * 512:(j + 1) * 512], ocTb)

    if "moba" in parts:
        load_rows(Kbuf, Kb, Ksem, ka)
        load_v(va, 0)
        kmf, kmfb = load_const(kb, "kmf", kmean, [128, NB], F32)
        kmh, kmhb = kb.sb("kmh", [128, NB], BF16)
        P.op("dve", lambda e: e.tensor_copy(out=kmh[:], in_=kmf[:]), reads=[kmfb], writes=[kmhb])
        gsb, gsbb = kb.sb("gsb", [128, NB], F32)
        top8, top8b = kb.sb("top8", [128, 8], F32)
        thr, thrb = kb.sb("thr", [128, 1], F32)
        bias, biasb = kb.sb("bias", [128, NB], BF16)
        biasT_rot = Rot([kb.sb(f"biasT{i}", [64, 512], BF16) for i in range(2)])
        pxt = ps_x
        for j in range(NQG):
            qat, qab = load_q(qa, j)
            bT, bTb = biasT_rot.next()
            for qs in range(4):
                own = 2 * j + qs // 2
                P.op("pe", lambda e, qs=qs, qat=qat: e.matmul(ps_x[:, 0:NB], lhsT=qat[:, qs * 128:(qs + 1) * 128], rhs=kmh[:, :], start=True, stop=True),
                     reads=[qab, kmhb], writes=[ps_xb])
                P.op("dve", lambda e: e.tensor_copy(out=gsb[:], in_=ps_x[:, 0:NB]), reads=[ps_xb], writes=[gsbb])
                P.op("dve", lambda e, own=own: e.memset(gsb[:, own:NB], -30000.0), writes=[gsbb])
                P.op("dve", lambda e: e.max(out=top8[:], in_=gsb[:]), reads=[gsbb], writes=[top8b])
                P.op("dve", lambda e: e.tensor_scalar_max(out=thr[:], in0=top8[:, 2:3], scalar1=-20000.0), reads=[top8b], writes=[thrb])
                P.op("dve", lambda e: e.tensor_scalar(out=bias[:], in0=gsb[:], scalar1=thr[:, 0:1], scalar2=-NEG, op0=ALU.is_ge, op1=ALU.mult),
                     reads=[gsbb, thrb], writes=[biasb])
                P.op("dve", lambda e, own=own: e.memset(bias[:, own:own + 1], -NEG), writes=[biasb])
                P.op("pe", lambda e, qs=qs: e.transpose(out=ps_t[0:NB, qs, :], in_=bias[:, 0:NB], identity=ident[:, :]),
                     reads=[biasb, identb], writes=[ps_tb])
                P.op("dve", lambda e, qs=qs, bT=bT: e.tensor_scalar_add(out=bT[0:NB, qs * 128:(qs + 1) * 128], in0=ps_t[0:NB, qs, :], scalar1=NEG),
                     reads=[ps_tb], **({"writes": [bTb]} if qs == 0 else {"pwrites": [bTb]}))

            def mk(kt, c0, c1, qat=qat, qab=qab, bT=bT, bTb=bTb):
                return [(Kbuf[:, kt * 128:(kt + 1) * 128], qat[:, c0:c1], [Kb, qab], c0, c1),
                        (Eoh[0:NB, kt // 2, :], bT[0:NB, c0:c1], [Eb, bTb], c0, c1)]
            attn_group(causal_blocks(j, mk, [Vb]), 128.0 ** -0.5, oaT[:, j * 512:(j + 1) * 512], oaTb)

    if "dil" in parts:
        WMAX = [w // 128 for w, d in DIL]
        MOFF = [0, 2, 7]
        kw_rot = [Rot(kb.slots(f"kw{g}_", 2, [128, (WMAX[g] + 4) * 128], BF16, dma=True)) for g in range(3)]
        vw_rot = [Rot(kb.slots(f"vw{g}_", 2, [128, WMAX[g] + 4, 128], BF16, dma=True)) for g in range(3)]
        for j in range(NQG):
            blocks = []
            for g in range(3):
                W = WMAX[g]
                lo = max(0, 4 * j - W)
                hi = 4 * j + 4
                n = hi - lo
                kw, kwb, kwsem = kw_rot[g].next()
                vw, vwb, vwsem = vw_rot[g].next()
                kb.load(kw[:, 0:n * 128], kd[g][:, lo * 128:hi * 128], kwb, kwsem)
                v3 = vd.rearrange("(kt p) d -> p kt d", p=128)
                kb.load(vw[:, 0:n, :], v3[:, lo:hi, g * 128:(g + 1) * 128], vwb, vwsem)
                qt, qtb = load_q(qd[g], j)
                for kt in range(lo, hi):
                    qlo = max(0, kt - 4 * j)
                    qhi = min(3, kt + W - 4 * j)
                    if qhi < qlo:
                        continue
                    c0, c1 = 128 * qlo, 128 * (qhi + 1)
                    off_lo = 4 * j + qlo - kt
                    cnt = qhi - qlo + 1
                    terms = [(kw[:, (kt - lo) * 128:(kt - lo + 1) * 128], qt[:, c0:c1], [kwb, qtb], c0, c1),
                             (ident[:, :], dm[:, MOFF[g] + off_lo:MOFF[g] + off_lo + cnt, :], [identb, dmb], c0, c1)]
                    blocks.append({"c0": c0, "c1": c1, "s_terms": terms, "v_ap": vw[:, kt - lo, :], "v_reads": [vwb]})
            attn_group(blocks, 128.0 ** -0.5, obT[:, j * 512:(j + 1) * 512], obTb)
    return kb.finish()


class NormRes:
    def __init__(self, kb, ident, identb, ps_tr):
        self.kb = kb
        self.xs, self.xsb = kb.sb("n_xs", [128, 2048], BF16)
        self.ss, self.ssb = kb.sb("n_ss", [128, 1], F32)
        self.rs, self.rsb = kb.sb("n_rs", [128, 1], F32)
        self.eps, self.epsb = kb.sb("n_eps", [128, 1], F32)
        kb.P.op("dve", lambda e: e.memset(self.eps[:], EPS), writes=[self.epsb])
        self.ident, self.identb = ident, identb
        self.ps_tr = ps_tr

    def rstd(self, x_ap, xb, rows):
        P = self.kb.P
        xs, xsb, ss, ssb, rs, rsb = self.xs, self.xsb, self.ss, self.ssb, self.rs, self.rsb
        P.op("act", lambda e: e.activation(out=xs[0:rows, :], in_=x_ap, func=AF.Square, accum_out=ss[0:rows, :]),
             reads=[xb], writes=[xsb, ssb])
        rstd_ops(P, rs[0:rows, :], rsb, ss[0:rows, :], ssb, 2048.0, self.eps[0:rows, :], self.epsb)

    def norm_T(self, x_ap, xb, rows, hT, hTb, col0, gb, gbb, partial):
        P = self.kb.P
        xs, xsb, rs, rsb = self.xs, self.xsb, self.rs, self.rsb
        self.rstd(x_ap, xb, rows)
        P.op("act", lambda e: e.activation(out=xs[0:rows, :], in_=x_ap, func=AF.Copy, scale=rs[0:rows, 0:1]),
             reads=[xb, rsb], writes=[xsb])
        ident, identb = self.ident, self.identb
        for half in range(2):
            pt, ptb = self.ps_tr.next()
            for k in range(8):
                kc = half * 8 + k
                P.op("pe", lambda e, k=k, kc=kc, pt=pt: e.transpose(out=pt[:, k, 0:rows], in_=xs[0:rows, kc * 128:(kc + 1) * 128],
                                                                   identity=ident[0:rows, 0:rows]),
                     reads=[xsb, identb], pwrites=[ptb])
            P.op("dve", lambda e, half=half, pt=pt: e.tensor_tensor(out=hT[:, half * 8:(half + 1) * 8, col0:col0 + rows],
                                                                    in0=pt[:, :, 0:rows], in1=gb[:, half * 8:(half + 1) * 8, 0:rows],
                                                                    op=ALU.mult),
                 reads=[ptb, gbb], **({"pwrites": [hTb]} if partial else {"writes": [hTb]}))
            partial = True


class AttnRes:
    def __init__(self, kb, ones, onesb, n_s=2, n_o=2):
        self.kb = kb
        self.ones, self.onesb = ones, onesb
        self.prot = Rot([kb.sb(f"pt{i}", [128, 512], BF16) for i in range(3)])
        self.ps_s = Rot([kb.ps(f"pS{i}", [128, 512], F32) for i in range(n_s)])
        self.ps_o = Rot([kb.ps(f"pO{i}", [128, 512], F32) for i in range(n_o)])
        self.ps_d = Rot([kb.ps(f"pD{i}", [128, 512], F32) for i in range(n_o)])
        self.rden_rot = Rot([kb.sb(f"rden{i}", [128, 512], F32) for i in range(2)])

    def group(self, blocks, scale, fin):
        P = self.kb.P
        ones, onesb = self.ones, self.onesb
        po, pob = self.ps_o.next()
        pd, pdb = self.ps_d.next()
        P.op("dve", lambda e: e.memset(po[:], 0.0), writes=[pob])
        P.op("dve", lambda e: e.memset(pd[:], 0.0), writes=[pdb])

        def emit_s(bk):
            psS, psSb = self.ps_s.next()
            n = len(bk["s_terms"])
            for i, (lhsT, rhs, rd, cc0, cc1) in enumerate(bk["s_terms"]):
                P.op("pe", lambda e, lhsT=lhsT, rhs=rhs, i=i, cc0=cc0, cc1=cc1: e.matmul(
                    psS[:, cc0:cc1], lhsT=lhsT, rhs=rhs, start=(i == 0), stop=(i == n - 1)),
                    reads=rd, **({"writes": [psSb]} if i == 0 else {"pwrites": [psSb]}))
            return psS, psSb

        depth = len(self.ps_s.items) - 1
        pend = [emit_s(bk) for bk in blocks[:depth]]
        for bi, bk in enumerate(blocks):
            if depth == 0:
                pend.append(emit_s(bk))
            psS, psSb = pend.pop(0)
            if depth and bi + depth < len(blocks):
                pend.append(emit_s(blocks[bi + depth]))
            c0, c1 = bk["c0"], bk["c1"]
            pt, ptb = self.prot.next()
            P.op("act", lambda e, psS=psS, pt=pt, c0=c0, c1=c1: e.activation(out=pt[:, c0:c1], in_=psS[:, c0:c1], func=AF.Exp, scale=scale),
                 reads=[psSb], writes=[ptb])
            P.op("pe", lambda e, pt=pt, c0=c0, c1=c1, v=bk["v_ap"]: e.matmul(po[:, c0:c1], lhsT=v, rhs=pt[:, c0:c1], start=False, stop=True,
                                                                              skip_group_check=True),
                 reads=[ptb] + bk["v_reads"], pwrites=[pob])
            P.op("pe", lambda e, pt=pt, c0=c0, c1=c1: e.matmul(pd[:, c0:c1], lhsT=ones[:, :], rhs=pt[:, c0:c1], start=False, stop=True,
                                                               skip_group_check=True),
                 reads=[ptb, onesb], pwrites=[pdb])
        rd_, rdb = self.rden_rot.next()
        P.op("dve", lambda e: e.reciprocal(out=rd_[:], in_=pd[:]), reads=[pdb], writes=[rdb])
        fin(po, pob, rd_, rdb)


def mm_group(P, pm_ap, pmb, terms):
    n = len(terms)
    for i, (lhsT, rhs, rd) in enumerate(terms):
        P.op("pe", lambda e, lhsT=lhsT, rhs=rhs, i=i: e.matmul(pm_ap, lhsT=lhsT, rhs=rhs, start=(i == 0), stop=(i == n - 1)),
             reads=rd, **({"writes": [pmb]} if i == 0 else {"pwrites": [pmb]}))


def build_C(TOK=4096, TB=512, stop=99):
    kb = KB()
    P = kb.P
    NT = TB // 128
    x = kb.din("x", [TOK, 2048], F32)
    oaT = kb.din("oaT", [4, 128, TOK], BF16)
    obT = kb.din("obT", [4, 128, TOK], BF16)
    ocT = kb.din("ocT", [8, 128, TOK], BF16)
    mem = kb.din("mem", [256, 2048], F32)
    gmix = kb.din("gmix_b", [128, 16, 128], F32)
    gmem = kb.din("gmem_b", [128, 16, 128], F32)
    gmkv = kb.din("gmkv_b", [128, 16, 128], F32)
    w_g = kb.din("w_gates", [2048, 6144], BF16)
    w_pa = kb.din("w_pa", [512, 2048], BF16)
    w_pb = kb.din("w_pb", [512, 2048], BF16)
    w_pc = kb.din("w_pc", [1024, 2048], BF16)
    w_o = kb.din("w_o", [2048, 2048], BF16)
    w_xq = kb.din("w_xq", [2048, 512], BF16)
    w_xk = kb.din("w_xk", [2048, 512], BF16)
    w_xv = kb.din("w_xv", [2048, 512], BF16)
    w_xo = kb.din("w_xo", [512, 2048], BF16)
    ident_d = kb.din("ident", [128, 128], BF16)
    ones_d = kb.din("ones", [128, 128], BF16)
    x2, x2b = kb.dout("x2", [TOK, 2048], F32)

    ident, identb = load_const(kb, "idents", ident_d, [128, 128], BF16)
    ones, onesb = load_const(kb, "oness", ones_d, [128, 128], BF16)
    ps_tr = Rot([kb.ps(f"ptr{i}", [128, 8, 128], BF16) for i in range(2)])
    ps_mm = Rot([kb.ps(f"pmm{i}", [128, 512], F32) for i in range(3)])
    NR = NormRes(kb, ident, identb, ps_tr)
    AR = AttnRes(kb, ones, onesb, n_s=1, n_o=1)
    wrot = Rot(kb.slots("w", 3, [128, 16, 512], BF16, dma=True))
    gb, gbb = kb.sb("gb", [128, 16, 128], F32)
    gbsem = P.dsem("gb")
    xres, xresb = kb.sb("xres", [128, NT, 2048], F32)
    xsem = P.dsem("xres")
    hT, hTb = kb.sb("hT", [128, 16, TB], BF16)
    mT, mTb = kb.sb("mT", [128, 16, TB], BF16)
    oin, oinb = kb.sb("oin", [128, 16, TB], BF16)
    oinsem = P.dsem("oin")
    acc = [kb.sb(f"acc{i}", [128, 512], F32) for i in range(4)]
    sg_rot = Rot([kb.sb(f"sg{i}", [128, 512], F32) for i in range(2)])
    tmp_rot = Rot([kb.sb(f"tmp{i}", [128, 512], F32) for i in range(2)])
    qxT, qxTb = kb.sb("qxT", [128, 4, TB], BF16)
    omT, omTb = kb.sb("omT", [128, 4, TB], BF16)
    kmT, kmTb = kb.sb("kmT", [128, 4, 256], BF16)
    vm, vmb = kb.sb("vm", [128, 2, 512], BF16)
    memT, memTb = kb.sb("memT", [128, 16, 256], BF16)

    kb.load(gb[:], gmkv, gbb, gbsem)
    for t in range(2):
        kb.load(xres[:, t, :], mem[t * 128:(t + 1) * 128, :], xresb, xsem)
        NR.norm_T(xres[:, t, :], xresb, 128, memT, memTb, t * 128, gb, gbb, partial=(t > 0))
    wt, wb = wload(kb, wrot, w_xk, 2048, 0, 512)
    for h in range(4):
        pm, pmb = ps_mm.next()
        mm_group(P, pm[:, 0:256], pmb, [(wt[:, kc, h * 128:(h + 1) * 128], memT[:, kc, :], [wb, memTb]) for kc in range(16)])
        P.op("act", lambda e, pm=pm, h=h: e.copy(out=kmT[:, h, :], in_=pm[:, 0:256]), reads=[pmb], pwrites=[kmTb])
    wt, wb = wload(kb, wrot, w_xv, 2048, 0, 512)
    for t in range(2):
        pm, pmb = ps_mm.next()
        mm_group(P, pm[:, :], pmb, [(memT[:, kc, t * 128:(t + 1) * 128], wt[:, kc, :], [wb, memTb]) for kc in range(16)])
        P.op("act", lambda e, pm=pm, t=t: e.copy(out=vm[:, t, :], in_=pm[:, :]), reads=[pmb], pwrites=[vmb])

    if stop < 1:
        return kb.finish()
    branches = [(w_pa, 512, 0), (w_pb, 512, 4), (w_pc, 1024, 8)]
    for blk in range(TOK // TB):
        t0 = blk * TB
        kb.load(gb[:], gmix, gbb, gbsem)
        for t in range(NT):
            P.op("sp", lambda e, t=t, t0=t0: e.dma_start(out=xres[:, t, :], in_=x[t0 + t * 128:t0 + (t + 1) * 128, :]),
                 dsem=xsem, **({"writes": [xresb]} if t == 0 else {"pwrites": [xresb]}))
        for (src, n, o0) in ((oaT, 4, 0), (obT, 4, 4), (ocT, 8, 8)):
            for c in range(n):
                P.op("sp", lambda e, src=src, c=c, o0=o0, t0=t0: e.dma_start(out=oin[:, o0 + c, :], in_=src[c, :, t0:t0 + TB]),
                     dsem=oinsem, **({"writes": [oinb]} if o0 + c == 0 else {"pwrites": [oinb]}))
        for t in range(NT):
            NR.norm_T(xres[:, t, :], xresb, 128, hT, hTb, t * 128, gb, gbb, partial=(t > 0))
        if stop < 2:
            continue
        for fg in range(4):
            for bi, (w_p, K, o0) in enumerate(branches):
                wg_t, wg_b = wload(kb, wrot, w_g, 2048, bi * 2048 + fg * 512, 512)
                wp_t, wp_b = wload(kb, wrot, w_p, K, fg * 512, 512)
                for f in range(4):
                    pg, pgb = ps_mm.next()
                    mm_group(P, pg[:, :], pgb, [(wg_t[:, kc, f * 128:(f + 1) * 128], hT[:, kc, :], [wg_b, hTb]) for kc in range(16)])
                    sg, sgb = sg_rot.next()
                    P.op("act", lambda e, sg=sg, pg=pg: e.activation(out=sg[:], in_=pg[:], func=AF.Sigmoid), reads=[pgb], writes=[sgb])
                    pp, ppb = ps_mm.next()
                    mm_group(P, pp[:, :], ppb, [(wp_t[:, kc, f * 128:(f + 1) * 128], oin[:, o0 + kc, :], [wp_b, oinb]) for kc in range(K // 128)])
                    a_t, a_b = acc[f]
                    if bi == 0:
                        P.op("dve", lambda e, a_t=a_t, pp=pp, sg=sg: e.tensor_tensor(out=a_t[:], in0=pp[:], in1=sg[:], op=ALU.mult),
                             reads=[ppb, sgb], writes=[a_b])
                    else:
                        tm, tmb = tmp_rot.next()
                        P.op("dve", lambda e, tm=tm, pp=pp, sg=sg: e.tensor_tensor(out=tm[:], in0=pp[:], in1=sg[:], op=ALU.mult),
                             reads=[ppb, sgb], writes=[tmb])
                        if bi == 1:
                            P.op("pool", lambda e, a_t=a_t, tm=tm: e.tensor_tensor(out=a_t[:], in0=a_t[:], in1=tm[:], op=ALU.add),
                                 reads=[a_b, tmb], writes=[a_b])
                        else:
                            ft = fg * 4 + f
                            P.op("pool", lambda e, a_t=a_t, tm=tm, ft=ft: e.tensor_tensor(out=mT[:, ft, :], in0=a_t[:], in1=tm[:], op=ALU.add),
                                 reads=[a_b, tmb], **({"writes": [mTb]} if ft == 0 else {"pwrites": [mTb]}))
        if stop < 3:
            continue
        for cb in range(4):
            wt, wb = wload(kb, wrot, w_o, 2048, cb * 512, 512)
            for t in range(NT):
                pm, pmb = ps_mm.next()
                mm_group(P, pm[:, :], pmb, [(mT[:, kc, t * 128:(t + 1) * 128], wt[:, kc, :], [wb, mTb]) for kc in range(16)])
                P.op("dve", lambda e, pm=pm, t=t, cb=cb: e.tensor_tensor(out=xres[:, t, cb * 512:(cb + 1) * 512], in0=pm[:],
                                                                        in1=xres[:, t, cb * 512:(cb + 1) * 512], op=ALU.add),
                     reads=[pmb, xresb], pwrites=[xresb])
        if stop < 4:
            continue
        kb.load(gb[:], gmem, gbb, gbsem)
        for t in range(NT):
            NR.norm_T(xres[:, t, :], xresb, 128, hT, hTb, t * 128, gb, gbb, partial=(t > 0))
        wt, wb = wload(kb, wrot, w_xq, 2048, 0, 512)
        for h in range(4):
            pm, pmb = ps_mm.next()
            mm_group(P, pm[:, :], pmb, [(wt[:, kc, h * 128:(h + 1) * 128], hT[:, kc, :], [wb, hTb]) for kc in range(16)])
            P.op("act", lambda e, pm=pm, h=h: e.copy(out=qxT[:, h, :], in_=pm[:]), reads=[pmb],
                 **({"writes": [qxTb]} if h == 0 else {"pwrites": [qxTb]}))
        for h in range(4):
            blocks = [{"c0": 0, "c1": 512,
                       "s_terms": [(kmT[:, h, kt * 128:(kt + 1) * 128], qxT[:, h, :], [kmTb, qxTb], 0, 512)],
                       "v_ap": vm[:, kt, h * 128:(h + 1) * 128], "v_reads": [vmb]} for kt in range(2)]

            def fin(po, pob, rd_, rdb, h=h):
                P.op("dve", lambda e: e.tensor_tensor(out=omT[:, h, :], in0=po[:], in1=rd_[:], op=ALU.mult),
                     reads=[pob, rdb], **({"writes": [omTb]} if h == 0 else {"pwrites": [omTb]}))
            AR.group(blocks, 128.0 ** -0.5, fin)
        for cb in range(4):
            wt, wb = wload(kb, wrot, w_xo, 512, cb * 512, 512)
            for t in range(NT):
                pm, pmb = ps_mm.next()
                mm_group(P, pm[:, :], pmb, [(omT[:, kc, t * 128:(t + 1) * 128], wt[:, kc, :], [wb, omTb]) for kc in range(4)])
                P.op("dve", lambda e, pm=pm, t=t, cb=cb: e.tensor_tensor(out=xres[:, t, cb * 512:(cb + 1) * 512], in0=pm[:],
                                                                        in1=xres[:, t, cb * 512:(cb + 1) * 512], op=ALU.add),
                     reads=[pmb, xresb], pwrites=[xresb])
        for t in range(NT):
            kb.store(x2[t0 + t * 128:t0 + (t + 1) * 128, :], xres[:, t, :], xresb, x2b, xsem)
    return kb.finish()


def build_D(TOK=4096, TB=512, final=False):
    kb = KB()
    P = kb.P
    NT = TB // 128
    NF = 43
    xh = kb.din("xh", [TOK + 2, 2048], F32)
    gffn = kb.din("gffn_b", [128, 16, 128], F32)
    w_up = kb.din("w_up", [2048, 11008], BF16)
    w_dn = kb.din("w_down", [5504, 2048], BF16)
    cw_d = kb.din("cw", [128, 86, 3], F32)
    cb_d = kb.din("cb", [128, 86], F32)
    ident_d = kb.din("ident", [128, 128], BF16)
    if final:
        gfin_d = kb.din("gfin_b", [128, 2048], F32)
    y, yb = kb.dout("y", [TOK, 2048], F32)

    ident, identb = load_const(kb, "idents", ident_d, [128, 128], BF16)
    cw, cwb = load_const(kb, "cws", cw_d, [128, 86, 3], F32)
    cbs, cbb = load_const(kb, "cbs", cb_d, [128, 86], F32)
    gb, gbb = load_const(kb, "gb", gffn, [128, 16, 128], F32)
    if final:
        gfin, gfinb = load_const(kb, "gfin", gfin_d, [128, 2048], F32)
    ps_tr = Rot([kb.ps(f"ptr{i}", [128, 8, 128], BF16) for i in range(2)])
    ps_u = Rot([kb.ps(f"pu{i}", [128, 512], F32) for i in range(3)])
    ps_h, ps_hb = kb.ps("ph", [128, 512], F32)
    ps_y = Rot([kb.ps(f"py{i}", [128, 512], F32) for i in range(2)])
    NR = NormRes(kb, ident, identb, ps_tr)
    wu_rot = Rot(kb.slots("wu", 3, [128, 16, 256], BF16, dma=True))
    wd_rot = Rot(kb.slots("wd", 2, [128, NF, 256], BF16, dma=True))
    xres, xresb = kb.sb("xres", [128, NT, 2048], F32)
    xsem = P.dsem("xres")
    xhal, xhalb = kb.sb("xhal", [2, 2048], F32)
    xhsem = P.dsem("xhal")
    hT, hTb = kb.sb("hT", [128, 16, TB + 2], BF16)
    aT, aTb = kb.sb("aT", [128, NF, TB], BF16)
    ucat_rot = Rot([kb.sb(f"ucat{i}", [128, TB + 2], F32) for i in range(2)])
    t_rot = Rot([kb.sb(f"tt{i}", [128, TB], F32) for i in range(2)])
    sgl_rot = Rot([kb.sb(f"sgl{i}", [128, TB], F32) for i in range(2)])

    usave, usb = kb.sb("usave", [128, 2 * NF, 2], F32)

    def conv_tile(wt, wb, f_local, fglob, first):
        pu, pub = ps_u.next()
        mm_group(P, pu[:, :], pub, [(wt[:, kc, f_local * 128:(f_local + 1) * 128], hT[:, kc, 2:TB + 2], [wb, hTb]) for kc in range(16)])
        uc, ucb = ucat_rot.next()
        tt, ttb = t_rot.next()
        P.op("act", lambda e: e.copy(out=uc[:, 2:TB + 2], in_=pu[:, :]), reads=[pub], writes=[ucb])
        if first:
            mm_group(P, ps_h[:, 0:2], ps_hb, [(wt[:, kc, f_local * 128:(f_local + 1) * 128], hT[:, kc, 0:2], [wb, hTb]) for kc in range(16)])
            P.op("act", lambda e: e.copy(out=uc[:, 0:2], in_=ps_h[:, 0:2]), reads=[ps_hb], pwrites=[ucb])
        else:
            P.op("act", lambda e: e.copy(out=uc[:, 0:2], in_=usave[:, fglob, :]), reads=[usb], pwrites=[ucb])
        P.op("act", lambda e: e.copy(out=usave[:, fglob, :], in_=uc[:, TB:TB + 2]), reads=[ucb], pwrites=[usb])
        P.op("act", lambda e: e.activation(out=tt[:, :], in_=pu[:, :], func=AF.Identity, bias=cbs[:, fglob:fglob + 1],
                                           scale=cw[:, fglob, 2:3]), reads=[pub, cbb, cwb], writes=[ttb])
        P.op("dve", lambda e: e.scalar_tensor_tensor(out=tt[:, :], in0=uc[:, 1:TB + 1], scalar=cw[:, fglob, 1:2], in1=tt[:, :],
                                                     op0=ALU.mult, op1=ALU.add), reads=[ucb, cwb, ttb], writes=[ttb])
        P.op("dve", lambda e: e.scalar_tensor_tensor(out=tt[:, :], in0=uc[:, 0:TB], scalar=cw[:, fglob, 0:1], in1=tt[:, :],
                                                     op0=ALU.mult, op1=ALU.add), reads=[ucb, cwb, ttb], writes=[ttb])
        return tt, ttb

    for blk in range(TOK // TB):
        t0 = blk * TB
        kb.load(xhal[:, :], xh[t0:t0 + 2, :], xhalb, xhsem)
        for t in range(NT):
            P.op("sp", lambda e, t=t, t0=t0: e.dma_start(out=xres[:, t, :], in_=xh[2 + t0 + t * 128:2 + t0 + (t + 1) * 128, :]),
                 dsem=xsem, **({"writes": [xresb]} if t == 0 else {"pwrites": [xresb]}))
        NR.norm_T(xhal[0:2, :], xhalb, 2, hT, hTb, 0, gb, gbb, partial=False)
        for t in range(NT):
            NR.norm_T(xres[:, t, :], xresb, 128, hT, hTb, 2 + t * 128, gb, gbb, partial=True)
        for c0 in range(0, 5504, 256):
            ncol = min(256, 5504 - c0)
            wg_t, wg_b = wload(kb, wu_rot, w_up, 2048, c0, ncol)
            wv_t, wv_b = wload(kb, wu_rot, w_up, 2048, 5504 + c0, ncol)
            for f in range(ncol // 128):
                ft = c0 // 128 + f
                tg, tgb = conv_tile(wg_t, wg_b, f, ft, blk == 0)
                sl_, slb = sgl_rot.next()
                P.op("act", lambda e, sl_=sl_, tg=tg: e.activation(out=sl_[:, :], in_=tg[:, :], func=AF.Silu), reads=[tgb], writes=[slb])
                tv, tvb = conv_tile(wv_t, wv_b, f, NF + ft, blk == 0)
                P.op("pool", lambda e, sl_=sl_, tv=tv, ft=ft: e.tensor_tensor(out=aT[:, ft, :], in0=sl_[:, :], in1=tv[:, :], op=ALU.mult),
                     reads=[slb, tvb], **({"writes": [aTb]} if ft == 0 else {"pwrites": [aTb]}))
        for cb_ in range(8):
            wd_t, wd_b, wd_sem = wd_rot.next()
            src = w_dn.rearrange("(kc p) n -> p kc n", p=128)
            for k0 in range(0, NF, 11):
                k1 = min(NF, k0 + 11)
                P.op("sp", lambda e, k0=k0, k1=k1, wd_t=wd_t, cb_=cb_: e.dma_start(out=wd_t[:, k0:k1, :], in_=src[:, k0:k1, cb_ * 256:(cb_ + 1) * 256]),
                     dsem=wd_sem, **({"writes": [wd_b]} if k0 == 0 else {"pwrites": [wd_b]}))
            for t in range(NT):
                py, pyb = ps_y.next()
                mm_group(P, py[:, 0:256], pyb, [(aT[:, kc, t * 128:(t + 1) * 128], wd_t[:, kc, :], [wd_b, aTb]) for kc in range(NF)])
                P.op("dve", lambda e, py=py, t=t, cb_=cb_: e.tensor_tensor(out=xres[:, t, cb_ * 256:(cb_ + 1) * 256], in0=py[:, 0:256],
                                                                          in1=xres[:, t, cb_ * 256:(cb_ + 1) * 256], op=ALU.add),
                     reads=[pyb, xresb], pwrites=[xresb])
        for t in range(NT):
            if final:
                NR.rstd(xres[:, t, :], xresb, 128)
                P.op("dve", lambda e, t=t: e.scalar_tensor_tensor(out=xres[:, t, :], in0=xres[:, t, :], scalar=NR.rs[:, 0:1], in1=gfin[:, :],
                                                                  op0=ALU.mult, op1=ALU.mult), reads=[xresb, NR.rsb, gfinb], pwrites=[xresb])
            kb.store(y[t0 + t * 128:t0 + (t + 1) * 128, :], xres[:, t, :], xresb, yb, xsem)
    return kb.finish()


_NC_CACHE = {}


def _get_nc(key, fn):
    if key not in _NC_CACHE:
        _NC_CACHE[key] = fn()
    return _NC_CACHE[key]


W_NAMES = ("w_in", "w_uq", "w_ukv", "w_pa", "w_pb", "w_pc", "w_o", "w_xq", "w_xk", "w_xv", "w_xo", "w_up", "w_down")


def kernel(**inp):
    inp = {k: np.asarray(v) for k, v in inp.items()}
    S, NQ = 16384, 4096
    wb = dict(zip(W_NAMES, cast_weights([inp[k] for k in W_NAMES])))
    p128, p64, ident, ones = perm_consts()
    aconst = attn_consts(S)
    tabs = [rope_tables(np.arange(qt * NQ, (qt + 1) * NQ)) for qt in range(4)]
    xcur = np.ascontiguousarray(inp["x"].reshape(8, NQ, 2048))
    for l in range(2):
        WA = prep_A_weights(wb, l)
        ncA = _get_nc("A", lambda: build_A(TOK=NQ, TB=1024))
        common = {"gmix_b": gbcast(inp["g_mix"][l]), "gcq": gcol(inp["g_cq"][l], 12), "gckv": gcol(inp["g_ckv"][l], 4),
                  "p128": p128, "p64": p64, "ident": ident, "ones": ones}
        for k in ("w_qk", "w_v", "w_lat", "w_uqn", "w_uqr", "w_ukn", "w_ukvv"):
            common[k] = WA[k]
        ims = []
        for c in range(8):
            cH, sH, cR, sR = tabs[c % 4]
            ims.append({**common, "x": xcur[c], "cosH": cH, "sinH": sH, "cosR": cR, "sinR": sR})
        rA = run_spmd(ncA, ims)
        def cat_b(name, b, axis):
            return np.concatenate([rA[4 * b + qt][name] for qt in range(4)], axis=axis)
        ims = []
        for b in range(2):
            QK = cat_b("qkT", b, 2)
            VAB = cat_b("vab", b, 0)
            KM = cat_b("kmean", b, 2)
            QN = cat_b("qnT", b, 2)
            QR = cat_b("qrT", b, 2)
            KN = cat_b("knT", b, 2)
            KR = cat_b("krT", b, 1)
            VC = cat_b("vc", b, 0)
            kr2 = np.ascontiguousarray(np.concatenate([KR, KR], 0))
            for c4 in range(4):
                ims.append({
                    "qn": np.ascontiguousarray(QN[2 * c4:2 * c4 + 2]), "qr": np.ascontiguousarray(QR[c4]),
                    "kn": np.ascontiguousarray(KN[2 * c4:2 * c4 + 2]), "kr2": kr2,
                    "vc": np.ascontiguousarray(VC[:, 256 * c4:256 * c4 + 256]),
                    "qa": np.ascontiguousarray(QK[c4]), "ka": np.ascontiguousarray(QK[4 + c4]),
                    "va": np.ascontiguousarray(VAB[:, 128 * c4:128 * c4 + 128]),
                    "kmean": np.ascontiguousarray(KM[:, c4, :]),
                    "qd": np.ascontiguousarray(np.stack([QK[8 + 4 * g + c4] for g in range(3)])),
                    "kd": np.ascontiguousarray(np.stack([QK[20 + 4 * g + c4] for g in range(3)])),
                    "vd": np.ascontiguousarray(np.concatenate([VAB[:, 512 + (4 * g + c4) * 128:512 + (4 * g + c4 + 1) * 128] for g in range(3)], 1)),
                    **aconst})
        del rA
        ncB = _get_nc("B", lambda: build_B(S=S))
        rB = run_spmd(ncB, ims)
        ncC = _get_nc("C", lambda: build_C(TOK=NQ, TB=512))
        commonC = {"gmix_b": gbcast(inp["g_mix"][l]), "gmem_b": gbcast(inp["g_mem"][l]), "gmkv_b": gbcast(inp["g_memkv"][l]),
                   "w_gates": WA["w_gates"], "w_pa": wb["w_pa"][l], "w_pb": wb["w_pb"][l], "w_pc": wb["w_pc"][l], "w_o": wb["w_o"][l],
                   "w_xq": wb["w_xq"][l], "w_xk": wb["w_xk"][l], "w_xv": wb["w_xv"][l], "w_xo": wb["w_xo"][l],
                   "ident": ident, "ones": ones}
        ims = []
        for c in range(8):
            b, qt = c // 4, c % 4
            sl = slice(qt * NQ, (qt + 1) * NQ)
            ims.append({**commonC, "x": xcur[c], "mem": np.ascontiguousarray(inp["mem"][b]),
                        "oaT": np.ascontiguousarray(np.stack([rB[4 * b + c4]["oaT"][:, sl] for c4 in range(4)])),
                        "obT": np.ascontiguousarray(np.stack([rB[4 * b + c4]["obT"][:, sl] for c4 in range(4)])),
                        "ocT": np.ascontiguousarray(np.stack([rB[4 * b + h // 2]["ocT"][h % 2][:, sl] for h in range(8)]))})
        del rB
        rC = run_spmd(ncC, ims)
        final = l == 1
        ncD = _get_nc("Df" if final else "D", lambda: build_D(TOK=NQ, TB=512, final=final))
        commonD = {"gffn_b": gbcast(inp["g_ffn"][l]), "w_up": wb["w_up"][l], "w_down": wb["w_down"][l],
                   "cw": np.ascontiguousarray(inp["conv_w"][l].T.reshape(86, 128, 3).transpose(1, 0, 2)),
                   "cb": np.ascontiguousarray(inp["conv_b"][l].reshape(86, 128).T), "ident": ident}
        if final:
            commonD["gfin_b"] = np.ascontiguousarray(np.broadcast_to(inp["g_final"][None, :], (128, 2048)))
        ims = []
        for c in range(8):
            halo = rC[c - 1]["x2"][-2:] if c % 4 else np.zeros((2, 2048), np.float32)
            ims.append({**commonD, "xh": np.ascontiguousarray(np.concatenate([halo, rC[c]["x2"]], 0))})
        del rC
        rD = run_spmd(ncD, ims)
        xcur = np.stack([rD[c]["y"] for c in range(8)])
        del rD
    return np.ascontiguousarray(xcur.reshape(2, S, 2048)).astype(np.float32)
```

```python
import math


import contextlib
import numpy as np
import concourse.bass as bass
import concourse.mybir as mybir
from concourse.bass_utils import run_bass_kernel_spmd

F32 = mybir.dt.float32
BF16 = mybir.dt.bfloat16
ALU = mybir.AluOpType
AF = mybir.ActivationFunctionType
AX = mybir.AxisListType


class Buf:
    __slots__ = ("name", "writers", "readers", "war", "excl", "base")

    def __init__(self, name, excl=False):
        self.name = name
        self.base = None
        self.excl = excl
        self.writers = []
        self.readers = []
        self.war = []


class Op:
    __slots__ = ("eng", "fn", "deps", "sig", "idx", "is_dma", "dsem")

    def __init__(self, eng, fn, is_dma=False):
        self.eng = eng
        self.fn = fn
        self.deps = []
        self.sig = False
        self.idx = None
        self.is_dma = is_dma
        self.dsem = None


class DmaSem:
    def __init__(self, name):
        self.name = name
        self.handle = None
        self.count = 0


ENGS = ("pe", "act", "dve", "pool", "sp")


class Prog:
    def __init__(self):
        self.ops = {e: [] for e in ENGS}
        self.dsems = []

    def dsem(self, name):
        d = DmaSem(name)
        self.dsems.append(d)
        return d

    def op(self, eng, fn, reads=(), writes=(), pwrites=(), dsem=None):
        o = Op(eng, fn, is_dma=dsem is not None)
        o.dsem = dsem
        deps = []
        rmw = [b for b in pwrites if any(b is r for r in reads)]
        for b in reads:
            deps.extend(b.writers)
            if b.excl:
                deps.extend(r for r in b.readers if r.eng != eng)
        for b in writes:
            deps.extend(b.readers)
            deps.extend(b.writers)
            deps.extend(b.war)
        for b in pwrites:
            deps.extend(b.readers)
            deps.extend(b.war)
            if b.base is not None:
                deps.append(b.base)
        seen = set()
        for d in deps:
            if d is o or id(d) in seen:
                continue
            if d.eng == "pe" and eng == "pe" and not d.is_dma:
                continue
            seen.add(id(d))
            o.deps.append((d, d.dsem.count if d.is_dma else None))
            d.sig = True
        if dsem is not None:
            dsem.count += 16
            o.idx = dsem.count
        for b in reads:
            if not any(b is r for r in rmw):
                b.readers.append(o)
        for b in writes:
            b.war = []
            b.readers = []
            b.writers = [o]
            b.base = o
        for b in pwrites:
            if any(b is r for r in rmw):
                b.war = b.war + b.readers
                b.readers = []
                b.writers.append(o)
            elif b.readers:
                b.war = b.war + b.readers
                b.readers = []
                b.writers = [o]
                b.base = None
            else:
                b.writers.append(o)
        self.ops[eng].append(o)
        return o

    def emit(self, nc, stack):
        esem = {}
        for e in ENGS:
            n = 0
            for o in self.ops[e]:
                if o.is_dma:
                    pass
                elif o.sig:
                    n += 1
                    o.idx = n
            esem[e] = stack.enter_context(nc.semaphore("s_" + e))
        for d in self.dsems:
            if d.count:
                d.handle = stack.enter_context(nc.semaphore("d_" + d.name))
        block = stack.enter_context(nc.Block())
        ops = self.ops

        def replay(ename, eng):
            known = {}
            for o in ops[ename]:
                need = {}
                for d, dv in o.deps:
                    if d.is_dma:
                        key, val = d.dsem.handle, dv
                    else:
                        key, val = esem[d.eng], d.idx
                    if val > known.get(key, 0) and val > need.get(key, 0):
                        need[key] = val
                for key, val in need.items():
                    eng.wait_ge(key, val)
                    known[key] = val
                ins = o.fn(eng)
                if ins is None:
                    continue
                if o.is_dma:
                    ins.then_inc(o.dsem.handle, 16)
                elif o.sig:
                    ins.then_inc(esem[ename], 1)

        @block.tensor
        def _(eng):
            replay("pe", eng)

        @block.scalar
        def _(eng):
            replay("act", eng)

        @block.vector
        def _(eng):
            replay("dve", eng)

        @block.gpsimd
        def _(eng):
            replay("pool", eng)

        @block.sync
        def _(eng):
            replay("sp", eng)
            for d in self.dsems:
                if d.count:
                    eng.wait_ge(d.handle, d.count)


class KB:
    def __init__(self):
        self.nc = bass.Bass("TRN2", target_bir_lowering=False)
        self.P = Prog()
        self.st = contextlib.ExitStack()
        self.bufs = {}
        self.nps = 0
        self.outs = []

    def din(self, name, shape, dt):
        return self.nc.dram_tensor(name, list(shape), dt, kind="ExternalInput").ap()

    def dout(self, name, shape, dt):
        ap = self.nc.dram_tensor(name, list(shape), dt, kind="ExternalOutput").ap()
        b = Buf(name)
        self.outs.append(b)
        return ap, b

    def sb(self, name, shape, dt):
        t = self.st.enter_context(self.nc.sbuf_tensor(name, list(shape), dt))
        return t, Buf(name)

    def ps(self, name, shape, dt):
        t = self.st.enter_context(self.nc.psum_tensor(name, list(shape), dt))
        return t, Buf(name, excl=True)

    def slots(self, name, n, shape, dt, dma=False):
        res = []
        for i in range(n):
            t, b = self.sb(f"{name}{i}", shape, dt)
            res.append((t, b, self.P.dsem(f"{name}{i}") if dma else None))
        return res

    def load(self, out_ap, in_ap, buf, dsem, eng="sp", reads=()):
        return self.P.op(eng, lambda e: e.dma_start(out=out_ap, in_=in_ap), reads=list(reads), writes=[buf], dsem=dsem)

    def store(self, out_ap, in_ap, src_buf, obuf, dsem, eng="sp"):
        return self.P.op(eng, lambda e: e.dma_start(out=out_ap, in_=in_ap), reads=[src_buf], pwrites=[obuf], dsem=dsem)

    def finish(self):
        self.P.op("sp", lambda e: None, reads=self.outs)
        self.P.emit(self.nc, self.st)
        self.st.close()
        return self.nc


class Rot:
    def __init__(self, items):
        self.items = items
        self.i = 0

    def next(self):
        it = self.items[self.i % len(self.items)]
        self.i += 1
        return it


def run_spmd(nc, in_maps):
    res = run_bass_kernel_spmd(nc, in_maps, core_ids=list(range(len(in_maps))))
    return res.results


import ml_dtypes
NPBF = ml_dtypes.bfloat16


def rope_tables(pos):
    pos = pos.astype(np.float32)
    invH = np.exp(np.arange(0, 128, 2, dtype=np.float32) * np.float32(-math.log(10000.0) / 128)).astype(np.float32)
    invR = np.exp(np.arange(0, 64, 2, dtype=np.float32) * np.float32(-math.log(10000.0) / 64)).astype(np.float32)
    angH = (pos[None, :] * invH[:, None]).astype(np.float32)
    angR = (pos[None, :] * invR[:, None]).astype(np.float32)
    cH = np.cos(angH.astype(np.float64)).astype(np.float32)
    sH = np.sin(angH.astype(np.float64)).astype(np.float32)
    cR = np.cos(angR.astype(np.float64)).astype(np.float32)
    sR = np.sin(angR.astype(np.float64)).astype(np.float32)
    cosH = np.concatenate([cH, cH], 0)
    sinH = np.concatenate([-sH, sH], 0)
    cosR = np.concatenate([cR, cR, cR, cR], 0)
    sinR = np.concatenate([-sR, sR, -sR, sR], 0)
    return [np.ascontiguousarray(a) for a in (cosH, sinH, cosR, sinR)]


def perm_consts():
    m = np.arange(128)
    p128 = np.zeros((128, 128), np.float32)
    p128[(m + 64) % 128, m] = 1
    p64 = np.zeros((128, 128), np.float32)
    p64[(m // 64) * 64 + ((m % 64) + 32) % 64, m] = 1
    return p128.astype(NPBF), p64.astype(NPBF), np.eye(128, dtype=np.float32).astype(NPBF), np.ones((128, 128), np.float32).astype(NPBF)


def gcol(g, n):
    return np.ascontiguousarray(g.reshape(n, 128).T)


def gbcast(g):
    return np.ascontiguousarray(np.broadcast_to(g.reshape(16, 128).T[:, :, None], (128, 16, 128)))


def prep_A_weights(wb, l):
    w_in = wb["w_in"][l]
    o = [0, 512, 1024, 1536, 3072, 4608, 6144, 7680, 8192, 8256, 14400]
    qa, ka, va, qb, kb_, vb, cq, ckv, kr, gates = [w_in[:, o[i]:o[i + 1]] for i in range(10)]
    w_uq = wb["w_uq"][l].reshape(1536, 8, 192)
    w_ukv = wb["w_ukv"][l].reshape(512, 8, 256)
    return {
        "w_qk": np.ascontiguousarray(np.concatenate([qa, ka, qb, kb_], 1)),
        "w_v": np.ascontiguousarray(np.concatenate([va, vb], 1)),
        "w_lat": np.ascontiguousarray(np.concatenate([cq, ckv, kr], 1)),
        "w_uqn": np.ascontiguousarray(w_uq[:, :, :128].reshape(1536, 1024)),
        "w_uqr": np.ascontiguousarray(w_uq[:, :, 128:].reshape(1536, 512)),
        "w_ukn": np.ascontiguousarray(w_ukv[:, :, :128].reshape(512, 1024)),
        "w_ukvv": np.ascontiguousarray(w_ukv[:, :, 128:].reshape(512, 1024)),
        "w_gates": np.ascontiguousarray(gates),
    }


CAST_F = 2048


def build_cast(ntiles):
    kb = KB()
    src = kb.din("src", [ntiles, 128, CAST_F], F32)
    dst, dstb = kb.dout("dst", [ntiles, 128, CAST_F], BF16)
    ins = Rot(kb.slots("ci", 3, [128, CAST_F], F32, dma=True))
    outs = Rot(kb.slots("co", 3, [128, CAST_F], BF16, dma=True))
    engs = ["dve", "act", "pool"]
    for t in range(ntiles):
        it, ib, isem = ins.next()
        ot, ob, osem = outs.next()
        kb.load(it[:], src[t], ib, isem)
        e = engs[t % 3]
        if e == "act":
            kb.P.op("act", lambda e_, ot=ot, it=it: e_.copy(out=ot[:], in_=it[:]), reads=[ib], writes=[ob])
        else:
            kb.P.op(e, lambda e_, ot=ot, it=it: e_.tensor_copy(out=ot[:], in_=it[:]), reads=[ib], writes=[ob])
        kb.store(dst[t], ot[:], ob, dstb, osem, eng="sp")
    return kb.finish()


def cast_weights(arrs):
    flat = np.concatenate([a.reshape(-1) for a in arrs])
    per = 8 * 128 * CAST_F
    n = -(-flat.size // per) * per
    pad = np.zeros(n, np.float32)
    pad[:flat.size] = flat
    nt = n // per
    sh = pad.reshape(8, nt, 128, CAST_F)
    nc = build_cast(nt)
    res = run_spmd(nc, [{"src": sh[c]} for c in range(8)])
    out = np.concatenate([r["dst"].reshape(-1) for r in res])
    outs = []
    o = 0
    for a in arrs:
        outs.append(out[o:o + a.size].reshape(a.shape))
        o += a.size
    return outs


EPS = 1e-6


def rstd_ops(P, out_ap, out_buf, in_ap, in_buf, n, eps_ap, eps_buf, partial=False):
    kw = {"pwrites": [out_buf]} if partial else {"writes": [out_buf]}
    P.op("act", lambda e: e.activation(out=out_ap, in_=in_ap, func=AF.Sqrt, bias=eps_ap, scale=1.0 / n),
         reads=[in_buf, eps_buf], **kw)
    P.op("dve", lambda e: e.reciprocal(out=out_ap, in_=out_ap), reads=[out_buf], **kw)


def norm_transpose(kb, x_ap, rows, hT, hT_buf, col0, gb, gb_buf, ident, ident_buf, xs_rot, ps_tr, partial, eps_ap, eps_buf):
    P = kb.P
    (xt, xb, xsem), (xs, xsb), (sq, sqb), (ss, ssb), (rs, rsb) = xs_rot.next()
    kb.load(xt[0:rows, :], x_ap, xb, xsem)
    P.op("act", lambda e: e.activation(out=sq[0:rows, :], in_=xt[0:rows, :], func=AF.Square, accum_out=ss[0:rows, :]),
         reads=[xb], writes=[sqb, ssb])
    rstd_ops(P, rs[0:rows, :], rsb, ss[0:rows, :], ssb, 2048.0, eps_ap[0:rows, :], eps_buf)
    P.op("act", lambda e: e.activation(out=xs[0:rows, :], in_=xt[0:rows, :], func=AF.Copy, scale=rs[0:rows, 0:1]),
         reads=[xb, rsb], writes=[xsb])
    for half in range(2):
        pt, ptb = ps_tr.next()
        for k in range(8):
            kc = half * 8 + k
            P.op("pe", lambda e, k=k, kc=kc, pt=pt: e.transpose(out=pt[:, k, 0:rows], in_=xs[0:rows, kc * 128:(kc + 1) * 128],
                                                               identity=ident[0:rows, 0:rows]),
                 reads=[xsb, ident_buf], pwrites=[ptb])
        P.op("dve", lambda e, half=half, pt=pt: e.tensor_tensor(out=hT[:, half * 8:(half + 1) * 8, col0:col0 + rows],
                                                                in0=pt[:, :, 0:rows], in1=gb[:, half * 8:(half + 1) * 8, 0:rows],
                                                                op=ALU.mult),
             reads=[ptb, gb_buf], **({"pwrites": [hT_buf]} if partial else {"writes": [hT_buf]}))
        partial = True


def make_xs_rot(kb, n=2):
    items = []
    for i in range(n):
        xt, xb = kb.sb(f"xt{i}", [128, 2048], F32)
        xsem = kb.P.dsem(f"xt{i}")
        items.append(((xt, xb, xsem), kb.sb(f"xs{i}", [128, 2048], BF16), kb.sb(f"xsq{i}", [128, 2048], BF16),
                      kb.sb(f"xss{i}", [128, 1], F32), kb.sb(f"xrs{i}", [128, 1], F32)))
    return Rot(items)


def load_const(kb, name, ap, shape, dt):
    t, b = kb.sb(name, shape, dt)
    kb.load(t[:], ap, b, kb.P.dsem(name))
    return t, b


def wload(kb, wrot, w_ap, K, c0, nc_):
    wt, wb, wsem = wrot.next()
    KC = K // 128
    src = w_ap.rearrange("(kc p) n -> p kc n", p=128)
    for k0 in range(0, KC, 4):
        kb.P.op("sp", lambda e, k0=k0: e.dma_start(out=wt[:, k0:k0 + 4, 0:nc_], in_=src[:, k0:k0 + 4, c0:c0 + nc_]),
                dsem=wsem, **({"writes": [wb]} if k0 == 0 else {"pwrites": [wb]}))
    return wt, wb


def build_A(TOK=4096, TB=1024, stop=99, dbg=99):
    kb = KB()
    P = kb.P
    NTG = TB // 512
    x = kb.din("x", [TOK, 2048], F32)
    gmix = kb.din("gmix_b", [128, 16, 128], F32)
    w_qk = kb.din("w_qk", [2048, 4096], BF16)
    w_v = kb.din("w_v", [2048, 2048], BF16)
    w_lat = kb.din("w_lat", [2048, 2112], BF16)
    gcq = kb.din("gcq", [128, 12], F32)
    gckv = kb.din("gckv", [128, 4], F32)
    w_uqn = kb.din("w_uqn", [1536, 1024], BF16)
    w_uqr = kb.din("w_uqr", [1536, 512], BF16)
    w_ukn = kb.din("w_ukn", [512, 1024], BF16)
    w_ukvv = kb.din("w_ukvv", [512, 1024], BF16)
    cosH = kb.din("cosH", [128, TOK], F32)
    sinH = kb.din("sinH", [128, TOK], F32)
    cosR = kb.din("cosR", [128, TOK], F32)
    sinR = kb.din("sinR", [128, TOK], F32)
    p128_d = kb.din("p128", [128, 128], BF16)
    p64_d = kb.din("p64", [128, 128], BF16)
    ident_d = kb.din("ident", [128, 128], BF16)
    ones_d = kb.din("ones", [128, 128], BF16)

    qkT, qkTb = kb.dout("qkT", [32, 128, TOK], BF16)
    vab, vabb = kb.dout("vab", [TOK, 2048], BF16)
    kmean, kmeanb = kb.dout("kmean", [128, 4, TOK // 256], F32)
    qnT, qnTb = kb.dout("qnT", [8, 128, TOK], BF16)
    qrT, qrTb = kb.dout("qrT", [4, 128, TOK], BF16)
    knT, knTb = kb.dout("knT", [8, 128, TOK], BF16)
    krT, krTb = kb.dout("krT", [64, TOK], BF16)
    vc, vcb = kb.dout("vc", [TOK, 1024], BF16)

    gb, gbb = load_const(kb, "gb", gmix, [128, 16, 128], F32)
    gq, gqb = load_const(kb, "gq", gcq, [128, 12], F32)
    gk, gkb = load_const(kb, "gk", gckv, [128, 4], F32)
    p128, p128b = load_const(kb, "p128s", p128_d, [128, 128], BF16)
    p64, p64b = load_const(kb, "p64s", p64_d, [128, 128], BF16)
    ident, identb = load_const(kb, "idents", ident_d, [128, 128], BF16)
    ones, onesb = load_const(kb, "oness", ones_d, [128, 128], BF16)

    epst, epsb = kb.sb("epst", [128, 1], F32)
    P.op("dve", lambda e: e.memset(epst[:], EPS), writes=[epsb])
    hT, hTb = kb.sb("hT", [128, 16, TB], BF16)
    cqT, cqTb = kb.sb("cqT", [128, 12, TB], BF16)
    ckT, ckTb = kb.sb("ckT", [128, 4, TB], BF16)
    sqk, sqkb = kb.sb("sqk", [128, 4, TB], BF16)
    rq, rqb = kb.sb("rq", [128, TB], F32)
    rk, rkb = kb.sb("rk", [128, TB], F32)
    rkc, rkcb = kb.sb("rkc", [128, TB // 128], F32)
    tabs = [kb.sb(n, [128, TB], F32) for n in ("cH", "sH", "cR", "sR")]
    tsem = [P.dsem(n) for n in ("cH", "sH", "cR", "sR")]
    km, kmb = kb.sb("km", [128, 4, TOK // 256], F32)

    xs_rot = make_xs_rot(kb)
    wrot = Rot(kb.slots("w", 3, [128, 16, 512], BF16, dma=True))
    ps_tr = Rot([kb.ps(f"ptr{i}", [128, 8, 128], BF16) for i in range(2)])
    ps_mm = Rot([kb.ps(f"pmm{i}", [128, 512], F32) for i in range(3)])
    ps_sw = Rot([kb.ps(f"psw{i}", [128, 512], F32) for i in range(2)])
    ps_ss, ps_ssb = kb.ps("pss", [128, 512], F32)
    raw_rot = Rot([kb.sb(f"raw{i}", [128, 512], BF16) for i in range(2)])
    t1_rot = Rot([kb.sb(f"t1_{i}", [128, 512], F32) for i in range(2)])
    t2_rot = Rot([kb.sb(f"t2_{i}", [128, 512], F32) for i in range(2)])
    sq_rot = Rot([kb.sb(f"sqt{i}", [128, 512], BF16) for i in range(2)])
    ost = Rot(kb.slots("ost", 4, [128, 512], BF16, dma=True))

    def rope_epi(pm, pmb, tg, cos_i, sin_i, perm, permb, out_ap, obuf, pre_scale=None, ksum=None, rows=128):
        cs, csb = tabs[cos_i]
        sn, snb = tabs[sin_i]
        sl = slice(tg * 512, (tg + 1) * 512)
        raw, rawb = raw_rot.next()
        t1, t1b = t1_rot.next()
        t2, t2b = t2_rot.next()
        psw, pswb = ps_sw.next()
        if pre_scale is None:
            P.op("act", lambda e: e.copy(out=raw[0:rows, :], in_=pm[0:rows, :]), reads=[pmb], writes=[rawb])
            P.op("dve", lambda e: e.tensor_tensor(out=t1[0:rows, :], in0=pm[0:rows, :], in1=cs[0:rows, sl], op=ALU.mult),
                 reads=[pmb, csb], writes=[t1b])
        else:
            rt, rtb = pre_scale
            P.op("dve", lambda e: e.tensor_tensor(out=t2[0:rows, :], in0=pm[0:rows, :], in1=rt[0:rows, sl], op=ALU.mult),
                 reads=[pmb, rtb], writes=[t2b])
            P.op("act", lambda e: e.copy(out=raw[0:rows, :], in_=t2[0:rows, :]), reads=[t2b], writes=[rawb])
            P.op("dve", lambda e: e.tensor_tensor(out=t1[0:rows, :], in0=t2[0:rows, :], in1=cs[0:rows, sl], op=ALU.mult),
                 reads=[t2b, csb], writes=[t1b])
        def part2():
            P.op("pe", lambda e: e.matmul(psw[0:rows, :], lhsT=perm[0:rows, 0:rows], rhs=raw[0:rows, :], start=True, stop=True),
                 reads=[rawb, permb], writes=[pswb])
            P.op("dve", lambda e: e.tensor_tensor(out=t2[0:rows, :], in0=psw[0:rows, :], in1=sn[0:rows, sl], op=ALU.mult),
                 reads=[pswb, snb], writes=[t2b])
            P.op("pool", lambda e: e.tensor_tensor(out=t1[0:rows, :], in0=t1[0:rows, :], in1=t2[0:rows, :], op=ALU.add),
                 reads=[t1b, t2b], writes=[t1b])
            ot, ob, osem = ost.next()
            P.op("act", lambda e: e.copy(out=ot[0:rows, :], in_=t1[0:rows, :]), reads=[t1b], writes=[ob])
            if ksum is not None:
                P.op("dve", lambda e: e.tensor_reduce(out=ksum, in_=t1[:, :].rearrange("p (b t) -> p b t", t=256),
                                                      axis=AX.X, op=ALU.add), reads=[t1b], pwrites=[kmb])
            kb.store(out_ap, ot[0:rows, :], ob, obuf, osem)
        return part2

    def scale_epi(pm, pmb, tg, rt, rtb, out_ap, obuf):
        sl = slice(tg * 512, (tg + 1) * 512)
        ot, ob, osem = ost.next()
        P.op("dve", lambda e: e.tensor_tensor(out=ot[:, :], in0=pm[:, :], in1=rt[:, sl], op=ALU.mult),
             reads=[pmb, rtb], writes=[ob])
        kb.store(out_ap, ot[:, :], ob, obuf, osem)

    def gemm_fm(w_ap, K, ncols, aT, aTb, epi):
        KC = K // 128
        pending = []
        for c0 in range(0, ncols, 512):
            nc_ = min(512, ncols - c0)
            wt, wb = wload(kb, wrot, w_ap, K, c0, nc_)
            for f0 in range(0, nc_, 128):
                rows = min(128, nc_ - f0)
                for tg in range(NTG):
                    pm, pmb = ps_mm.next()
                    for kc in range(KC):
                        P.op("pe", lambda e, pm=pm, wt=wt, kc=kc, f0=f0, rows=rows, tg=tg: e.matmul(
                            pm[0:rows, :], lhsT=wt[:, kc, f0:f0 + rows], rhs=aT[:, kc, tg * 512:(tg + 1) * 512],
                            start=(kc == 0), stop=(kc == KC - 1)),
                            reads=[wb, aTb], **({"writes": [pmb]} if kc == 0 else {"pwrites": [pmb]}))
                    for d_ in pending:
                        d_()
                    pending.clear()
                    d_ = epi((c0 + f0) // 128, tg, pm, pmb, rows)
                    if d_ is not None:
                        pending.append(d_)
        for d_ in pending:
            d_()

    def gemm_tm(w_ap, K, ncols, aT, aTb, epi):
        KC = K // 128
        for c0 in range(0, ncols, 512):
            wt, wb = wload(kb, wrot, w_ap, K, c0, 512)
            for t in range(TB // 128):
                pm, pmb = ps_mm.next()
                for kc in range(KC):
                    P.op("pe", lambda e, pm=pm, wt=wt, kc=kc, t=t: e.matmul(
                        pm[:, :], lhsT=aT[:, kc, t * 128:(t + 1) * 128], rhs=wt[:, kc, 0:512],
                        start=(kc == 0), stop=(kc == KC - 1)),
                        reads=[wb, aTb], **({"writes": [pmb]} if kc == 0 else {"pwrites": [pmb]}))
                epi(c0 // 512, t, pm, pmb)

    for blk in range(TOK // TB):
        t0 = blk * TB
        for (tt, tb_), d, sem in zip(tabs, (cosH, sinH, cosR, sinR), tsem):
            kb.load(tt[:], d[:, t0:t0 + TB], tb_, sem)
        for t in range(TB // 128):
            norm_transpose(kb, x[t0 + t * 128:t0 + (t + 1) * 128, :], 128, hT, hTb, t * 128, gb, gbb, ident, identb,
                           xs_rot, ps_tr, partial=(t > 0), eps_ap=epst, eps_buf=epsb)

        if stop < 1:
            continue
        def lat_epi(ft, tg, pm, pmb, rows):
            sl = slice(tg * 512, (tg + 1) * 512)
            if dbg < 1:
                return
            if ft >= 16 and dbg < 4:
                return
            if ft < 16:
                isq = ft < 12
                sq, sqb = sq_rot.next()
                if isq:
                    P.op("act", lambda e: e.activation(out=sq[:, :], in_=pm[:, :], func=AF.Square), reads=[pmb], writes=[sqb])
                    P.op("dve", lambda e: e.tensor_scalar(out=cqT[:, ft, sl], in0=pm[:, :], scalar1=gq[:, ft:ft + 1], scalar2=None,
                                                          op0=ALU.mult), reads=[pmb, gqb], pwrites=[cqTb])
                    sq_ap, sq_b = sq[:, :], sqb
                    first, last = ft == 0, ft == 11
                else:
                    f = ft - 12
                    P.op("act", lambda e: e.activation(out=sqk[:, f, sl], in_=pm[:, :], func=AF.Square), reads=[pmb], pwrites=[sqkb])
                    P.op("dve", lambda e: e.tensor_scalar(out=ckT[:, f, sl], in0=pm[:, :], scalar1=gk[:, f:f + 1], scalar2=None,
                                                          op0=ALU.mult), reads=[pmb, gkb], pwrites=[ckTb])
                    sq_ap, sq_b = sqk[:, f, sl], sqkb
                    first, last = f == 0, f == 3
                if dbg < 2:
                    return
                P.op("pe", lambda e: e.matmul(ps_ss[:, :], lhsT=ones[:, :], rhs=sq_ap, start=first, stop=last),
                     reads=[onesb, sq_b], **({"writes": [ps_ssb]} if first else {"pwrites": [ps_ssb]}))
                if last and dbg >= 3:
                    n = 1536.0 if isq else 512.0
                    rt, rtb = (rq, rqb) if isq else (rk, rkb)
                    rstd_ops(P, rt[:, sl], rtb, ps_ss[:, :], ps_ssb, n, epst[:, :], epsb, partial=True)
            else:
                rope_epi(pm, pmb, tg, 2, 3, p64, p64b, krT[:, t0 + tg * 512:t0 + (tg + 1) * 512], krTb, rows=64)()

        KC = 16
        lat_w = []
        for c0 in range(0, 2112, 512):
            nc_ = min(512, 2112 - c0)
            lat_w.append((c0, nc_))
        for tg in range(NTG):
            for c0, nc_ in lat_w:
                wt, wb = wload(kb, wrot, w_lat, 2048, c0, nc_)
                for f0 in range(0, nc_, 128):
                    rows = min(128, nc_ - f0)
                    pm, pmb = ps_mm.next()
                    for kc in range(KC):
                        P.op("pe", lambda e, pm=pm, wt=wt, kc=kc, f0=f0, rows=rows, tg=tg: e.matmul(
                            pm[0:rows, :], lhsT=wt[:, kc, f0:f0 + rows], rhs=hT[:, kc, tg * 512:(tg + 1) * 512],
                            start=(kc == 0), stop=(kc == KC - 1)),
                            reads=[wb, hTb], **({"writes": [pmb]} if kc == 0 else {"pwrites": [pmb]}))
                    lat_epi((c0 + f0) // 128, tg, pm, pmb, rows)
        if stop < 2:
            continue
        for t in range(TB // 128):
            pm, pmb = ps_mm.next()
            for f in range(4):
                P.op("pe", lambda e, pm=pm, f=f, t=t: e.matmul(pm[:, 0:1], lhsT=sqk[:, f, t * 128:(t + 1) * 128], rhs=ones[:, 0:1],
                                                               start=(f == 0), stop=(f == 3)),
                     reads=[sqkb, onesb], **({"writes": [pmb]} if f == 0 else {"pwrites": [pmb]}))
            rstd_ops(P, rkc[:, t:t + 1], rkcb, pm[:, 0:1], pmb, 512.0, epst[:, :], epsb, partial=True)

        if stop < 3:
            continue
        gemm_fm(w_uqn, 1536, 1024, cqT, cqTb,
                lambda ft, tg, pm, pmb, rows: scale_epi(pm, pmb, tg, rq, rqb, qnT[ft, :, t0 + tg * 512:t0 + (tg + 1) * 512], qnTb))
        gemm_fm(w_uqr, 1536, 512, cqT, cqTb,
                lambda ft, tg, pm, pmb, rows: rope_epi(pm, pmb, tg, 2, 3, p64, p64b,
                                                       qrT[ft, :, t0 + tg * 512:t0 + (tg + 1) * 512], qrTb, pre_scale=(rq, rqb)))
        gemm_fm(w_ukn, 512, 1024, ckT, ckTb,
                lambda ft, tg, pm, pmb, rows: scale_epi(pm, pmb, tg, rk, rkb, knT[ft, :, t0 + tg * 512:t0 + (tg + 1) * 512], knTb))

        def vc_epi(cb, t, pm, pmb):
            ot, ob, osem = ost.next()
            P.op("act", lambda e: e.activation(out=ot[:, :], in_=pm[:, :], func=AF.Copy, scale=rkc[:, t:t + 1]),
                 reads=[pmb, rkcb], writes=[ob])
            kb.store(vc[t0 + t * 128:t0 + (t + 1) * 128, cb * 512:(cb + 1) * 512], ot[:, :], ob, vcb, osem)
        gemm_tm(w_ukvv, 512, 1024, ckT, ckTb, vc_epi)

        if stop < 4:
            continue
        def qk_epi(ft, tg, pm, pmb, rows):
            ksum = None
            if 4 <= ft < 8:
                b0 = (t0 + tg * 512) // 256
                ksum = km[:, ft - 4, b0:b0 + 2]
            return rope_epi(pm, pmb, tg, 0, 1, p128, p128b, qkT[ft, :, t0 + tg * 512:t0 + (tg + 1) * 512], qkTb, ksum=ksum)
        gemm_fm(w_qk, 2048, 4096, hT, hTb, qk_epi)

        if stop < 5:
            continue
        def v_epi(cb, t, pm, pmb):
            ot, ob, osem = ost.next()
            P.op("act", lambda e: e.copy(out=ot[:, :], in_=pm[:, :]), reads=[pmb], writes=[ob])
            kb.store(vab[t0 + t * 128:t0 + (t + 1) * 128, cb * 512:(cb + 1) * 512], ot[:, :], ob, vabb, osem)
        gemm_tm(w_v, 2048, 2048, hT, hTb, v_epi)

    kmo, kmob = kb.sb("kmo", [128, 4, TOK // 256], F32)
    if stop < 4:
        return kb.finish()
    P.op("act", lambda e: e.mul(out=kmo[:], in_=km[:], mul=1.0 / 256), reads=[kmb], writes=[kmob])
    kb.store(kmean[:, :, :], kmo[:], kmob, kmeanb, P.dsem("kmo"))
    return kb.finish()


NEG = -32768.0
DIL = ((128, 1), (512, 4), (2048, 16))


def dil_masks():
    k = np.arange(128)[:, None]
    q = np.arange(128)[None, :]
    ms = []
    for w, d in DIL:
        for off in range(w // 128 + 1):
            dist = 128 * off + q - k
            ok = (dist >= 0) & (dist <= w) & (dist % d == 0)
            ms.append(np.where(ok, 0.0, NEG))
    return np.ascontiguousarray(np.stack(ms, 1).astype(np.float32)).astype(NPBF)


def attn_consts(S):
    k = np.arange(128)[:, None]
    q = np.arange(128)[None, :]
    tri = np.where(k <= q, 0.0, NEG).astype(np.float32).astype(NPBF)
    nb = S // 256
    E = np.zeros((64, nb, 128), np.float32)
    for n in range(nb):
        E[n, n, :] = 1
    return {"tri": tri, "Eoh": E.astype(NPBF), "dmask": dil_masks(),
            "ident": np.eye(128, dtype=np.float32).astype(NPBF), "ones": np.ones((128, 128), np.float32).astype(NPBF)}


def build_B(S=16384, parts=("mla", "moba", "dil")):
    kb = KB()
    P = kb.P
    NQG = S // 512
    NKT = S // 128
    NB = S // 256
    qn = kb.din("qn", [2, 128, S], BF16)
    qr = kb.din("qr", [128, S], BF16)
    kn = kb.din("kn", [2, 128, S], BF16)
    kr2 = kb.din("kr2", [128, S], BF16)
    vc = kb.din("vc", [S, 256], BF16)
    qa = kb.din("qa", [128, S], BF16)
    ka = kb.din("ka", [128, S], BF16)
    va = kb.din("va", [S, 128], BF16)
    kmean = kb.din("kmean", [128, NB], F32)
    qd = kb.din("qd", [3, 128, S], BF16)
    kd = kb.din("kd", [3, 128, S], BF16)
    vd = kb.din("vd", [S, 384], BF16)
    tri_d = kb.din("tri", [128, 128], BF16)
    E_d = kb.din("Eoh", [64, NB, 128], BF16)
    dm_d = kb.din("dmask", [128, 24, 128], BF16)
    ident_d = kb.din("ident", [128, 128], BF16)
    ones_d = kb.din("ones", [128, 128], BF16)
    ocT, ocTb = kb.dout("ocT", [2, 128, S], BF16)
    oaT, oaTb = kb.dout("oaT", [128, S], BF16)
    obT, obTb = kb.dout("obT", [128, S], BF16)

    tri, trib = load_const(kb, "tri_s", tri_d, [128, 128], BF16)
    ident, identb = load_const(kb, "ident_s", ident_d, [128, 128], BF16)
    ones, onesb = load_const(kb, "ones_s", ones_d, [128, 128], BF16)
    dm, dmb = load_const(kb, "dm_s", dm_d, [128, 24, 128], BF16)
    Eoh, Eb = load_const(kb, "E_s", E_d, [64, NB, 128], BF16)

    Kbuf, Kb = kb.sb("Kbuf", [128, S], BF16)
    Ksem = P.dsem("Kbuf")
    KRbuf, KRb = kb.sb("KRbuf", [128, S], BF16)
    KRsem = P.dsem("KRbuf")
    Vbuf, Vb = kb.sb("Vbuf", [128, NKT, 128], BF16)
    Vsem = P.dsem("Vbuf")
    qrot = Rot(kb.slots("qt", 6, [128, 512], BF16, dma=True))
    prot = Rot([kb.sb(f"pt{i}", [128, 512], BF16) for i in range(4)])
    NS = 3
    ps_s = Rot([kb.ps(f"pS{i}", [128, 512], F32) for i in range(NS)])
    ps_o = Rot([kb.ps(f"pO{i}", [128, 512], F32) for i in range(2)])
    ps_d = Rot([kb.ps(f"pD{i}", [128, 512], F32) for i in range(1)])

    ps_x, ps_xb = kb.ps("pX", [128, 512], F32)
    ps_t, ps_tb = kb.ps("pT", [128, 8, 128], BF16)
    rden_rot = Rot([kb.sb(f"rden{i}", [128, 512], F32) for i in range(2)])
    ost = Rot(kb.slots("ost", 3, [128, 512], BF16, dma=True))

    def load_rows(dst, dbuf, dsem, src_ap, nchunk=4):
        w = S // nchunk
        for i in range(nchunk):
            P.op("sp", lambda e, i=i: e.dma_start(out=dst[:, i * w:(i + 1) * w], in_=src_ap[:, i * w:(i + 1) * w]),
                 dsem=dsem, **({"writes": [dbuf]} if i == 0 else {"pwrites": [dbuf]}))

    def load_v(src_ap, c0):
        v3 = src_ap.rearrange("(kt p) d -> p kt d", p=128)
        for k0 in range(0, NKT, 8):
            P.op("sp", lambda e, k0=k0: e.dma_start(out=Vbuf[:, k0:k0 + 8, :], in_=v3[:, k0:k0 + 8, c0:c0 + 128]),
                 dsem=Vsem, **({"writes": [Vb]} if k0 == 0 else {"pwrites": [Vb]}))

    def load_q(src_ap, j):
        qt, qb_, qsem = qrot.next()
        kb.load(qt[:], src_ap[:, j * 512:(j + 1) * 512], qb_, qsem)
        return qt, qb_

    def attn_group(blocks, scale, out_ap, obuf):
        po, pob = ps_o.next()
        pd, pdb = ps_d.next()
        P.op("dve", lambda e: e.memset(po[:], 0.0), writes=[pob])
        P.op("dve", lambda e: e.memset(pd[:], 0.0), writes=[pdb])

        def emit_s(bk):
            psS, psSb = ps_s.next()
            n = len(bk["s_terms"])
            for i, (lhsT, rhs, rd, cc0, cc1) in enumerate(bk["s_terms"]):
                P.op("pe", lambda e, lhsT=lhsT, rhs=rhs, i=i, cc0=cc0, cc1=cc1: e.matmul(
                    psS[:, cc0:cc1], lhsT=lhsT, rhs=rhs, start=(i == 0), stop=(i == n - 1)),
                    reads=rd, **({"writes": [psSb]} if i == 0 else {"pwrites": [psSb]}))
            return psS, psSb

        depth = NS - 1
        pend = [emit_s(bk) for bk in blocks[:depth]]
        for bi, bk in enumerate(blocks):
            psS, psSb = pend.pop(0)
            if bi + depth < len(blocks):
                pend.append(emit_s(blocks[bi + depth]))
            c0, c1 = bk["c0"], bk["c1"]
            pt, ptb = prot.next()
            P.op("act", lambda e, psS=psS, pt=pt, c0=c0, c1=c1: e.activation(out=pt[:, c0:c1], in_=psS[:, c0:c1], func=AF.Exp, scale=scale),
                 reads=[psSb], writes=[ptb])
            P.op("pe", lambda e, pt=pt, c0=c0, c1=c1, v=bk["v_ap"]: e.matmul(po[:, c0:c1], lhsT=v, rhs=pt[:, c0:c1], start=False, stop=True,
                                                                              skip_group_check=True),
                 reads=[ptb] + bk["v_reads"], pwrites=[pob])
            P.op("pe", lambda e, pt=pt, c0=c0, c1=c1: e.matmul(pd[:, c0:c1], lhsT=ones[:, :], rhs=pt[:, c0:c1], start=False, stop=True,
                                                               skip_group_check=True),
                 reads=[ptb, onesb], pwrites=[pdb])
        rd_, rdb = rden_rot.next()
        ot, ob, osem = ost.next()
        P.op("dve", lambda e: e.reciprocal(out=rd_[:], in_=pd[:]), reads=[pdb], writes=[rdb])
        P.op("dve", lambda e: e.tensor_tensor(out=ot[:], in0=po[:], in1=rd_[:], op=ALU.mult), reads=[pob, rdb], writes=[ob])
        kb.store(out_ap, ot[:], ob, obuf, osem)

    def causal_blocks(j, mk_terms, vreads):
        blocks = []
        for kt in range(4 * (j + 1)):
            diag = kt >= 4 * j
            c0 = 128 * (kt - 4 * j) if diag else 0
            terms = mk_terms(kt, c0, 512)
            if diag:
                terms.append((ident[:, :], tri[:, :], [identb, trib], c0, c0 + 128))
            blocks.append({"c0": c0, "c1": 512, "s_terms": terms, "v_ap": Vbuf[:, kt, :], "v_reads": vreads})
        return blocks

    if "mla" in parts:
        load_rows(KRbuf, KRb, KRsem, kr2)
        for h in range(2):
            load_rows(Kbuf, Kb, Ksem, kn[h])
            load_v(vc, h * 128)
            hp = h * 64
            for j in range(NQG):
                qnt, qnb = load_q(qn[h], j)
                qrt, qrb = load_q(qr, j)

                def mk(kt, c0, c1, qnt=qnt, qnb=qnb, qrt=qrt, qrb=qrb):
                    return [(Kbuf[:, kt * 128:(kt + 1) * 128], qnt[:, c0:c1], [Kb, qnb], c0, c1),
                            (KRbuf[hp:hp + 64, kt * 128:(kt + 1) * 128], qrt[hp:hp + 64, c0:c1], [KRb, qrb], c0, c1)]
                attn_group(causal_blocks(j, mk, [Vb]), 192.0 ** -0.5, ocT[h, :, j * 512:(j + 1) * 512], ocTb)

    if "moba" in parts:
        load_rows(Kbuf, Kb, Ksem, ka)
        load_v(va, 0)
        kmf, kmfb = load_const(kb, "kmf", kmean, [128, NB], F32)
        kmh, kmhb = kb.sb("kmh", [128, NB], BF16)
        P.op("dve", lambda e: e.tensor_copy(out=kmh[:], in_=kmf[:]), reads=[kmfb], writes=[kmhb])
        gsb, gsbb = kb.sb("gsb", [128, NB], F32)
        top8, top8b = kb.sb("top8", [128, 8], F32)
        thr, thrb = kb.sb("thr", [128, 1], F32)
        bias, biasb = kb.sb("bias", [128, NB], BF16)
        biasT_rot = Rot([kb.sb(f"biasT{i}", [64, 512], BF16) for i in range(2)])
        pxt = ps_x
        for j in range(NQG):
            qat, qab = load_q(qa, j)
            bT, bTb = biasT_rot.next()
            for qs in range(4):
                own = 2 * j + qs // 2
                P.op("pe", lambda e, qs=qs, qat=qat: e.matmul(ps_x[:, 0:NB], lhsT=qat[:, qs * 128:(qs + 1) * 128], rhs=kmh[:, :], start=True, stop=True),
                     reads=[qab, kmhb], writes=[ps_xb])
                P.op("dve", lambda e: e.tensor_copy(out=gsb[:], in_=ps_x[:, 0:NB]), reads=[ps_xb], writes=[gsbb])
                P.op("dve", lambda e, own=own: e.memset(gsb[:, own:NB], -30000.0), writes=[gsbb])
                P.op("dve", lambda e: e.max(out=top8[:], in_=gsb[:]), reads=[gsbb], writes=[top8b])
                P.op("dve", lambda e: e.tensor_scalar_max(out=thr[:], in0=top8[:, 2:3], scalar1=-20000.0), reads=[top8b], writes=[thrb])
                P.op("dve", lambda e: e.tensor_scalar(out=bias[:], in0=gsb[:], scalar1=thr[:, 0:1], scalar2=-NEG, op0=ALU.is_ge, op1=ALU.mult),
                     reads=[gsbb, thrb], writes=[biasb])
                P.op("dve", lambda e, own=own: e.memset(bias[:, own:own + 1], -NEG), writes=[biasb])
                P.op("pe", lambda e, qs=qs: e.transpose(out=ps_t[0:NB, qs, :], in_=bias[:, 0:NB], identity=ident[:, :]),
                     reads=[biasb, identb], writes=[ps_tb])
                P.op("dve", lambda e, qs=qs, bT=bT: e.tensor_scalar_add(out=bT[0:NB, qs * 128:(qs + 1) * 128], in0=ps_t[0:NB, qs, :], scalar1=NEG),
                     reads=[ps_tb], **({"writes": [bTb]} if qs == 0 else {"pwrites": [bTb]}))

            def mk(kt, c0, c1, qat=qat, qab=qab, bT=bT, bTb=bTb):
                return [(Kbuf[:, kt * 128:(kt + 1) * 128], qat[:, c0:c1], [Kb, qab], c0, c1),
                        (Eoh[0:NB, kt // 2, :], bT[0:NB, c0:c1], [Eb, bTb], c0, c1)]
            attn_group(causal_blocks(j, mk, [Vb]), 128.0 ** -0.5, oaT[:, j * 512:(j + 1) * 512], oaTb)

    if "dil" in parts:
        WMAX = [w // 128 for w, d in DIL]
        MOFF = [0, 2, 7]
        kw_rot = [Rot(kb.slots(f"kw{g}_", 2, [128, (WMAX[g] + 4) * 128], BF16, dma=True)) for g in range(3)]
        vw_rot = [Rot(kb.slots(f"vw{g}_", 2, [128, WMAX[g] + 4, 128], BF16, dma=True)) for g in range(3)]
        for j in range(NQG):
            blocks = []
            for g in range(3):
                W = WMAX[g]
                lo = max(0, 4 * j - W)
                hi = 4 * j + 4
                n = hi - lo
                kw, kwb, kwsem = kw_rot[g].next()
                vw, vwb, vwsem = vw_rot[g].next()
                kb.load(kw[:, 0:n * 128], kd[g][:, lo * 128:hi * 128], kwb, kwsem)
                v3 = vd.rearrange("(kt p) d -> p kt d", p=128)
                kb.load(vw[:, 0:n, :], v3[:, lo:hi, g * 128:(g + 1) * 128], vwb, vwsem)
                qt, qtb = load_q(qd[g], j)
                for kt in range(lo, hi):
                    qlo = max(0, kt - 4 * j)
                    qhi = min(3, kt + W - 4 * j)
                    if qhi < qlo:
                        continue
                    c0, c1 = 128 * qlo, 128 * (qhi + 1)
                    off_lo = 4 * j + qlo - kt
                    cnt = qhi - qlo + 1
                    terms = [(kw[:, (kt - lo) * 128:(kt - lo + 1) * 128], qt[:, c0:c1], [kwb, qtb], c0, c1),
                             (ident[:, :], dm[:, MOFF[g] + off_lo:MOFF[g] + off_lo + cnt, :], [identb, dmb], c0, c1)]
                    blocks.append({"c0": c0, "c1": c1, "s_terms": terms, "v_ap": vw[:, kt - lo, :], "v_reads": [vwb]})
            attn_group(blocks, 128.0 ** -0.5, obT[:, j * 512:(j + 1) * 512], obTb)
    return kb.finish()


class NormRes:
    def __init__(self, kb, ident, identb, ps_tr):
        self.kb = kb
        self.xs, self.xsb = kb.sb("n_xs", [128, 2048], BF16)
        self.ss, self.ssb = kb.sb("n_ss", [128, 1], F32)
        self.rs, self.rsb = kb.sb("n_rs", [128, 1], F32)
        self.eps, self.epsb = kb.sb("n_eps", [128, 1], F32)
        kb.P.op("dve", lambda e: e.memset(self.eps[:], EPS), writes=[self.epsb])
        self.ident, self.identb = ident, identb
        self.ps_tr = ps_tr

    def rstd(self, x_ap, xb, rows):
        P = self.kb.P
        xs, xsb, ss, ssb, rs, rsb = self.xs, self.xsb, self.ss, self.ssb, self.rs, self.rsb
        P.op("act", lambda e: e.activation(out=xs[0:rows, :], in_=x_ap, func=AF.Square, accum_out=ss[0:rows, :]),
             reads=[xb], writes=[xsb, ssb])
        rstd_ops(P, rs[0:rows, :], rsb, ss[0:rows, :], ssb, 2048.0, self.eps[0:rows, :], self.epsb)

    def norm_T(self, x_ap, xb, rows, hT, hTb, col0, gb, gbb, partial):
        P = self.kb.P
        xs, xsb, rs, rsb = self.xs, self.xsb, self.rs, self.rsb
        self.rstd(x_ap, xb, rows)
        P.op("act", lambda e: e.activation(out=xs[0:rows, :], in_=x_ap, func=AF.Copy, scale=rs[0:rows, 0:1]),
             reads=[xb, rsb], writes=[xsb])
        ident, identb = self.ident, self.identb
        for half in range(2):
            pt, ptb = self.ps_tr.next()
            for k in range(8):
                kc = half * 8 + k
                P.op("pe", lambda e, k=k, kc=kc, pt=pt: e.transpose(out=pt[:, k, 0:rows], in_=xs[0:rows, kc * 128:(kc + 1) * 128],
                                                                   identity=ident[0:rows, 0:rows]),
                     reads=[xsb, identb], pwrites=[ptb])
            P.op("dve", lambda e, half=half, pt=pt: e.tensor_tensor(out=hT[:, half * 8:(half + 1) * 8, col0:col0 + rows],
                                                                    in0=pt[:, :, 0:rows], in1=gb[:, half * 8:(half + 1) * 8, 0:rows],
                                                                    op=ALU.mult),
                 reads=[ptb, gbb], **({"pwrites": [hTb]} if partial else {"writes": [hTb]}))
            partial = True


class AttnRes:
    def __init__(self, kb, ones, onesb, n_s=2, n_o=2):
        self.kb = kb
        self.ones, self.onesb = ones, onesb
        self.prot = Rot([kb.sb(f"pt{i}", [128, 512], BF16) for i in range(3)])
        self.ps_s = Rot([kb.ps(f"pS{i}", [128, 512], F32) for i in range(n_s)])
        self.ps_o = Rot([kb.ps(f"pO{i}", [128, 512], F32) for i in range(n_o)])
        self.ps_d = Rot([kb.ps(f"pD{i}", [128, 512], F32) for i in range(n_o)])
        self.rden_rot = Rot([kb.sb(f"rden{i}", [128, 512], F32) for i in range(2)])

    def group(self, blocks, scale, fin):
        P = self.kb.P
        ones, onesb = self.ones, self.onesb
        po, pob = self.ps_o.next()
        pd, pdb = self.ps_d.next()
        P.op("dve", lambda e: e.memset(po[:], 0.0), writes=[pob])
        P.op("dve", lambda e: e.memset(pd[:], 0.0), writes=[pdb])

        def emit_s(bk):
            psS, psSb = self.ps_s.next()
            n = len(bk["s_terms"])
            for i, (lhsT, rhs, rd, cc0, cc1) in enumerate(bk["s_terms"]):
                P.op("pe", lambda e, lhsT=lhsT, rhs=rhs, i=i, cc0=cc0, cc1=cc1: e.matmul(
                    psS[:, cc0:cc1], lhsT=lhsT, rhs=rhs, start=(i == 0), stop=(i == n - 1)),
                    reads=rd, **({"writes": [psSb]} if i == 0 else {"pwrites": [psSb]}))
            return psS, psSb

        depth = len(self.ps_s.items) - 1
        pend = [emit_s(bk) for bk in blocks[:depth]]
        for bi, bk in enumerate(blocks):
            if depth == 0:
                pend.append(emit_s(bk))
            psS, psSb = pend.pop(0)
            if depth and bi + depth < len(blocks):
                pend.append(emit_s(blocks[bi + depth]))
            c0, c1 = bk["c0"], bk["c1"]
            pt, ptb = self.prot.next()
            P.op("act", lambda e, psS=psS, pt=pt, c0=c0, c1=c1: e.activation(out=pt[:, c0:c1], in_=psS[:, c0:c1], func=AF.Exp, scale=scale),
                 reads=[psSb], writes=[ptb])
            P.op("pe", lambda e, pt=pt, c0=c0, c1=c1, v=bk["v_ap"]: e.matmul(po[:, c0:c1], lhsT=v, rhs=pt[:, c0:c1], start=False, stop=True,
                                                                              skip_group_check=True),
                 reads=[ptb] + bk["v_reads"], pwrites=[pob])
            P.op("pe", lambda e, pt=pt, c0=c0, c1=c1: e.matmul(pd[:, c0:c1], lhsT=ones[:, :], rhs=pt[:, c0:c1], start=False, stop=True,
                                                               skip_group_check=True),
                 reads=[ptb, onesb], pwrites=[pdb])
        rd_, rdb = self.rden_rot.next()
        P.op("dve", lambda e: e.reciprocal(out=rd_[:], in_=pd[:]), reads=[pdb], writes=[rdb])
        fin(po, pob, rd_, rdb)


def mm_group(P, pm_ap, pmb, terms):
    n = len(terms)
    for i, (lhsT, rhs, rd) in enumerate(terms):
        P.op("pe", lambda e, lhsT=lhsT, rhs=rhs, i=i: e.matmul(pm_ap, lhsT=lhsT, rhs=rhs, start=(i == 0), stop=(i == n - 1)),
             reads=rd, **({"writes": [pmb]} if i == 0 else {"pwrites": [pmb]}))


def build_C(TOK=4096, TB=512, stop=99):
    kb = KB()
    P = kb.P
    NT = TB // 128
    x = kb.din("x", [TOK, 2048], F32)
    oaT = kb.din("oaT", [4, 128, TOK], BF16)
    obT = kb.din("obT", [4, 128, TOK], BF16)
    ocT = kb.din("ocT", [8, 128, TOK], BF16)
    mem = kb.din("mem", [256, 2048], F32)
    gmix = kb.din("gmix_b", [128, 16, 128], F32)
    gmem = kb.din("gmem_b", [128, 16, 128], F32)
    gmkv = kb.din("gmkv_b", [128, 16, 128], F32)
    w_g = kb.din("w_gates", [2048, 6144], BF16)
    w_pa = kb.din("w_pa", [512, 2048], BF16)
    w_pb = kb.din("w_pb", [512, 2048], BF16)
    w_pc = kb.din("w_pc", [1024, 2048], BF16)
    w_o = kb.din("w_o", [2048, 2048], BF16)
    w_xq = kb.din("w_xq", [2048, 512], BF16)
    w_xk = kb.din("w_xk", [2048, 512], BF16)
    w_xv = kb.din("w_xv", [2048, 512], BF16)
    w_xo = kb.din("w_xo", [512, 2048], BF16)
    ident_d = kb.din("ident", [128, 128], BF16)
    ones_d = kb.din("ones", [128, 128], BF16)
    x2, x2b = kb.dout("x2", [TOK, 2048], F32)

    ident, identb = load_const(kb, "idents", ident_d, [128, 128], BF16)
    ones, onesb = load_const(kb, "oness", ones_d, [128, 128], BF16)
    ps_tr = Rot([kb.ps(f"ptr{i}", [128, 8, 128], BF16) for i in range(2)])
    ps_mm = Rot([kb.ps(f"pmm{i}", [128, 512], F32) for i in range(3)])
    NR = NormRes(kb, ident, identb, ps_tr)
    AR = AttnRes(kb, ones, onesb, n_s=1, n_o=1)
    wrot = Rot(kb.slots("w", 3, [128, 16, 512], BF16, dma=True))
    gb, gbb = kb.sb("gb", [128, 16, 128], F32)
    gbsem = P.dsem("gb")
    xres, xresb = kb.sb("xres", [128, NT, 2048], F32)
    xsem = P.dsem("xres")
    hT, hTb = kb.sb("hT", [128, 16, TB], BF16)
    mT, mTb = kb.sb("mT", [128, 16, TB], BF16)
    oin, oinb = kb.sb("oin", [128, 16, TB], BF16)
    oinsem = P.dsem("oin")
    acc = [kb.sb(f"acc{i}", [128, 512], F32) for i in range(4)]
    sg_rot = Rot([kb.sb(f"sg{i}", [128, 512], F32) for i in range(2)])
    tmp_rot = Rot([kb.sb(f"tmp{i}", [128, 512], F32) for i in range(2)])
    qxT, qxTb = kb.sb("qxT", [128, 4, TB], BF16)
    omT, omTb = kb.sb("omT", [128, 4, TB], BF16)
    kmT, kmTb = kb.sb("kmT", [128, 4, 256], BF16)
    vm, vmb = kb.sb("vm", [128, 2, 512], BF16)
    memT, memTb = kb.sb("memT", [128, 16, 256], BF16)

    kb.load(gb[:], gmkv, gbb, gbsem)
    for t in range(2):
        kb.load(xres[:, t, :], mem[t * 128:(t + 1) * 128, :], xresb, xsem)
        NR.norm_T(xres[:, t, :], xresb, 128, memT, memTb, t * 128, gb, gbb, partial=(t > 0))
    wt, wb = wload(kb, wrot, w_xk, 2048, 0, 512)
    for h in range(4):
        pm, pmb = ps_mm.next()
        mm_group(P, pm[:, 0:256], pmb, [(wt[:, kc, h * 128:(h + 1) * 128], memT[:, kc, :], [wb, memTb]) for kc in range(16)])
        P.op("act", lambda e, pm=pm, h=h: e.copy(out=kmT[:, h, :], in_=pm[:, 0:256]), reads=[pmb], pwrites=[kmTb])
    wt, wb = wload(kb, wrot, w_xv, 2048, 0, 512)
    for t in range(2):
        pm, pmb = ps_mm.next()
        mm_group(P, pm[:, :], pmb, [(memT[:, kc, t * 128:(t + 1) * 128], wt[:, kc, :], [wb, memTb]) for kc in range(16)])
        P.op("act", lambda e, pm=pm, t=t: e.copy(out=vm[:, t, :], in_=pm[:, :]), reads=[pmb], pwrites=[vmb])

    if stop < 1:
        return kb.finish()
    branches = [(w_pa, 512, 0), (w_pb, 512, 4), (w_pc, 1024, 8)]
    for blk in range(TOK // TB):
        t0 = blk * TB
        kb.load(gb[:], gmix, gbb, gbsem)
        for t in range(NT):
            P.op("sp", lambda e, t=t, t0=t0: e.dma_start(out=xres[:, t, :], in_=x[t0 + t * 128:t0 + (t + 1) * 128, :]),
                 dsem=xsem, **({"writes": [xresb]} if t == 0 else {"pwrites": [xresb]}))
        for (src, n, o0) in ((oaT, 4, 0), (obT, 4, 4), (ocT, 8, 8)):
            for c in range(n):
                P.op("sp", lambda e, src=src, c=c, o0=o0, t0=t0: e.dma_start(out=oin[:, o0 + c, :], in_=src[c, :, t0:t0 + TB]),
                     dsem=oinsem, **({"writes": [oinb]} if o0 + c == 0 else {"pwrites": [oinb]}))
        for t in range(NT):
            NR.norm_T(xres[:, t, :], xresb, 128, hT, hTb, t * 128, gb, gbb, partial=(t > 0))
        if stop < 2:
            continue
        for fg in range(4):
            for bi, (w_p, K, o0) in enumerate(branches):
                wg_t, wg_b = wload(kb, wrot, w_g, 2048, bi * 2048 + fg * 512, 512)
                wp_t, wp_b = wload(kb, wrot, w_p, K, fg * 512, 512)
                for f in range(4):
                    pg, pgb = ps_mm.next()
                    mm_group(P, pg[:, :], pgb, [(wg_t[:, kc, f * 128:(f + 1) * 128], hT[:, kc, :], [wg_b, hTb]) for kc in range(16)])
                    sg, sgb = sg_rot.next()
                    P.op("act", lambda e, sg=sg, pg=pg: e.activation(out=sg[:], in_=pg[:], func=AF.Sigmoid), reads=[pgb], writes=[sgb])
                    pp, ppb = ps_mm.next()
                    mm_group(P, pp[:, :], ppb, [(wp_t[:, kc, f * 128:(f + 1) * 128], oin[:, o0 + kc, :], [wp_b, oinb]) for kc in range(K // 128)])
                    a_t, a_b = acc[f]
                    if bi == 0:
                        P.op("dve", lambda e, a_t=a_t, pp=pp, sg=sg: e.tensor_tensor(out=a_t[:], in0=pp[:], in1=sg[:], op=ALU.mult),
                             reads=[ppb, sgb], writes=[a_b])
                    else:
                        tm, tmb = tmp_rot.next()
                        P.op("dve", lambda e, tm=tm, pp=pp, sg=sg: e.tensor_tensor(out=tm[:], in0=pp[:], in1=sg[:], op=ALU.mult),
                             reads=[ppb, sgb], writes=[tmb])
                        if bi == 1:
                            P.op("pool", lambda e, a_t=a_t, tm=tm: e.tensor_tensor(out=a_t[:], in0=a_t[:], in1=tm[:], op=ALU.add),
                                 reads=[a_b, tmb], writes=[a_b])
                        else:
                            ft = fg * 4 + f
                            P.op("pool", lambda e, a_t=a_t, tm=tm, ft=ft: e.tensor_tensor(out=mT[:, ft, :], in0=a_t[:], in1=tm[:], op=ALU.add),
                                 reads=[a_b, tmb], **({"writes": [mTb]} if ft == 0 else {"pwrites": [mTb]}))
        if stop < 3:
            continue
        for cb in range(4):
            wt, wb = wload(kb, wrot, w_o, 2048, cb * 512, 512)
            for t in range(NT):
                pm, pmb = ps_mm.next()
                mm_group(P, pm[:, :], pmb, [(mT[:, kc, t * 128:(t + 1) * 128], wt[:, kc, :], [wb, mTb]) for kc in range(16)])
                P.op("dve", lambda e, pm=pm, t=t, cb=cb: e.tensor_tensor(out=xres[:, t, cb * 512:(cb + 1) * 512], in0=pm[:],
                                                                        in1=xres[:, t, cb * 512:(cb + 1) * 512], op=ALU.add),
                     reads=[pmb, xresb], pwrites=[xresb])
        if stop < 4:
            continue
        kb.load(gb[:], gmem, gbb, gbsem)
        for t in range(NT):
            NR.norm_T(xres[:, t, :], xresb, 128, hT, hTb, t * 128, gb, gbb, partial=(t > 0))
        wt, wb = wload(kb, wrot, w_xq, 2048, 0, 512)
        for h in range(4):
            pm, pmb = ps_mm.next()
            mm_group(P, pm[:, :], pmb, [(wt[:, kc, h * 128:(h + 1) * 128], hT[:, kc, :], [wb, hTb]) for kc in range(16)])
            P.op("act", lambda e, pm=pm, h=h: e.copy(out=qxT[:, h, :], in_=pm[:]), reads=[pmb],
                 **({"writes": [qxTb]} if h == 0 else {"pwrites": [qxTb]}))
        for h in range(4):
            blocks = [{"c0": 0, "c1": 512,
                       "s_terms": [(kmT[:, h, kt * 128:(kt + 1) * 128], qxT[:, h, :], [kmTb, qxTb], 0, 512)],
                       "v_ap": vm[:, kt, h * 128:(h + 1) * 128], "v_reads": [vmb]} for kt in range(2)]

            def fin(po, pob, rd_, rdb, h=h):
                P.op("dve", lambda e: e.tensor_tensor(out=omT[:, h, :], in0=po[:], in1=rd_[:], op=ALU.mult),
                     reads=[pob, rdb], **({"writes": [omTb]} if h == 0 else {"pwrites": [omTb]}))
            AR.group(blocks, 128.0 ** -0.5, fin)
        for cb in range(4):
            wt, wb = wload(kb, wrot, w_xo, 512, cb * 512, 512)
            for t in range(NT):
                pm, pmb = ps_mm.next()
                mm_group(P, pm[:, :], pmb, [(omT[:, kc, t * 128:(t + 1) * 128], wt[:, kc, :], [wb, omTb]) for kc in range(4)])
                P.op("dve", lambda e, pm=pm, t=t, cb=cb: e.tensor_tensor(out=xres[:, t, cb * 512:(cb + 1) * 512], in0=pm[:],
                                                                        in1=xres[:, t, cb * 512:(cb + 1) * 512], op=ALU.add),
                     reads=[pmb, xresb], pwrites=[xresb])
        for t in range(NT):
            kb.store(x2[t0 + t * 128:t0 + (t + 1) * 128, :], xres[:, t, :], xresb, x2b, xsem)
    return kb.finish()


def build_D(TOK=4096, TB=512, final=False):
    kb = KB()
    P = kb.P
    NT = TB // 128
    NF = 43
    xh = kb.din("xh", [TOK + 2, 2048], F32)
    gffn = kb.din("gffn_b", [128, 16, 128], F32)
    w_up = kb.din("w_up", [2048, 11008], BF16)
    w_dn = kb.din("w_down", [5504, 2048], BF16)
    cw_d = kb.din("cw", [128, 86, 3], F32)
    cb_d = kb.din("cb", [128, 86], F32)
    ident_d = kb.din("ident", [128, 128], BF16)
    if final:
        gfin_d = kb.din("gfin_b", [128, 2048], F32)
    y, yb = kb.dout("y", [TOK, 2048], F32)

    ident, identb = load_const(kb, "idents", ident_d, [128, 128], BF16)
    cw, cwb = load_const(kb, "cws", cw_d, [128, 86, 3], F32)
    cbs, cbb = load_const(kb, "cbs", cb_d, [128, 86], F32)
    gb, gbb = load_const(kb, "gb", gffn, [128, 16, 128], F32)
    if final:
        gfin, gfinb = load_const(kb, "gfin", gfin_d, [128, 2048], F32)
    ps_tr = Rot([kb.ps(f"ptr{i}", [128, 8, 128], BF16) for i in range(2)])
    ps_u = Rot([kb.ps(f"pu{i}", [128, 512], F32) for i in range(3)])
    ps_h, ps_hb = kb.ps("ph", [128, 512], F32)
    ps_y = Rot([kb.ps(f"py{i}", [128, 512], F32) for i in range(2)])
    NR = NormRes(kb, ident, identb, ps_tr)
    wu_rot = Rot(kb.slots("wu", 3, [128, 16, 256], BF16, dma=True))
    wd_rot = Rot(kb.slots("wd", 2, [128, NF, 256], BF16, dma=True))
    xres, xresb = kb.sb("xres", [128, NT, 2048], F32)
    xsem = P.dsem("xres")
    xhal, xhalb = kb.sb("xhal", [2, 2048], F32)
    xhsem = P.dsem("xhal")
    hT, hTb = kb.sb("hT", [128, 16, TB + 2], BF16)
    aT, aTb = kb.sb("aT", [128, NF, TB], BF16)
    ucat_rot = Rot([kb.sb(f"ucat{i}", [128, TB + 2], F32) for i in range(2)])
    t_rot = Rot([kb.sb(f"tt{i}", [128, TB], F32) for i in range(2)])
    sgl_rot = Rot([kb.sb(f"sgl{i}", [128, TB], F32) for i in range(2)])

    usave, usb = kb.sb("usave", [128, 2 * NF, 2], F32)

    def conv_tile(wt, wb, f_local, fglob, first):
        pu, pub = ps_u.next()
        mm_group(P, pu[:, :], pub, [(wt[:, kc, f_local * 128:(f_local + 1) * 128], hT[:, kc, 2:TB + 2], [wb, hTb]) for kc in range(16)])
        uc, ucb = ucat_rot.next()
        tt, ttb = t_rot.next()
        P.op("act", lambda e: e.copy(out=uc[:, 2:TB + 2], in_=pu[:, :]), reads=[pub], writes=[ucb])
        if first:
            mm_group(P, ps_h[:, 0:2], ps_hb, [(wt[:, kc, f_local * 128:(f_local + 1) * 128], hT[:, kc, 0:2], [wb, hTb]) for kc in range(16)])
            P.op("act", lambda e: e.copy(out=uc[:, 0:2], in_=ps_h[:, 0:2]), reads=[ps_hb], pwrites=[ucb])
        else:
            P.op("act", lambda e: e.copy(out=uc[:, 0:2], in_=usave[:, fglob, :]), reads=[usb], pwrites=[ucb])
        P.op("act", lambda e: e.copy(out=usave[:, fglob, :], in_=uc[:, TB:TB + 2]), reads=[ucb], pwrites=[usb])
        P.op("act", lambda e: e.activation(out=tt[:, :], in_=pu[:, :], func=AF.Identity, bias=cbs[:, fglob:fglob + 1],
                                           scale=cw[:, fglob, 2:3]), reads=[pub, cbb, cwb], writes=[ttb])
        P.op("dve", lambda e: e.scalar_tensor_tensor(out=tt[:, :], in0=uc[:, 1:TB + 1], scalar=cw[:, fglob, 1:2], in1=tt[:, :],
                                                     op0=ALU.mult, op1=ALU.add), reads=[ucb, cwb, ttb], writes=[ttb])
        P.op("dve", lambda e: e.scalar_tensor_tensor(out=tt[:, :], in0=uc[:, 0:TB], scalar=cw[:, fglob, 0:1], in1=tt[:, :],
                                                     op0=ALU.mult, op1=ALU.add), reads=[ucb, cwb, ttb], writes=[ttb])
        return tt, ttb

    for blk in range(TOK // TB):
        t0 = blk * TB
        kb.load(xhal[:, :], xh[t0:t0 + 2, :], xhalb, xhsem)
        for t in range(NT):
            P.op("sp", lambda e, t=t, t0=t0: e.dma_start(out=xres[:, t, :], in_=xh[2 + t0 + t * 128:2 + t0 + (t + 1) * 128, :]),
                 dsem=xsem, **({"writes": [xresb]} if t == 0 else {"pwrites": [xresb]}))
        NR.norm_T(xhal[0:2, :], xhalb, 2, hT, hTb, 0, gb, gbb, partial=False)
        for t in range(NT):
            NR.norm_T(xres[:, t, :], xresb, 128, hT, hTb, 2 + t * 128, gb, gbb, partial=True)
        for c0 in range(0, 5504, 256):
            ncol = min(256, 5504 - c0)
            wg_t, wg_b = wload(kb, wu_rot, w_up, 2048, c0, ncol)
            wv_t, wv_b = wload(kb, wu_rot, w_up, 2048, 5504 + c0, ncol)
            for f in range(ncol // 128):
                ft = c0 // 128 + f
                tg, tgb = conv_tile(wg_t, wg_b, f, ft, blk == 0)
                sl_, slb = sgl_rot.next()
                P.op("act", lambda e, sl_=sl_, tg=tg: e.activation(out=sl_[:, :], in_=tg[:, :], func=AF.Silu), reads=[tgb], writes=[slb])
                tv, tvb = conv_tile(wv_t, wv_b, f, NF + ft, blk == 0)
                P.op("pool", lambda e, sl_=sl_, tv=tv, ft=ft: e.tensor_tensor(out=aT[:, ft, :], in0=sl_[:, :], in1=tv[:, :], op=ALU.mult),
                     reads=[slb, tvb], **({"writes": [aTb]} if ft == 0 else {"pwrites": [aTb]}))
        for cb_ in range(8):
            wd_t, wd_b, wd_sem = wd_rot.next()
            src = w_dn.rearrange("(kc p) n -> p kc n", p=128)
            for k0 in range(0, NF, 11):
                k1 = min(NF, k0 + 11)
                P.op("sp", lambda e, k0=k0, k1=k1, wd_t=wd_t, cb_=cb_: e.dma_start(out=wd_t[:, k0:k1, :], in_=src[:, k0:k1, cb_ * 256:(cb_ + 1) * 256]),
                     dsem=wd_sem, **({"writes": [wd_b]} if k0 == 0 else {"pwrites": [wd_b]}))
            for t in range(NT):
                py, pyb = ps_y.next()
                mm_group(P, py[:, 0:256], pyb, [(aT[:, kc, t * 128:(t + 1) * 128], wd_t[:, kc, :], [wd_b, aTb]) for kc in range(NF)])
                P.op("dve", lambda e, py=py, t=t, cb_=cb_: e.tensor_tensor(out=xres[:, t, cb_ * 256:(cb_ + 1) * 256], in0=py[:, 0:256],
                                                                          in1=xres[:, t, cb_ * 256:(cb_ + 1) * 256], op=ALU.add),
                     reads=[pyb, xresb], pwrites=[xresb])
        for t in range(NT):
            if final:
                NR.rstd(xres[:, t, :], xresb, 128)
                P.op("dve", lambda e, t=t: e.scalar_tensor_tensor(out=xres[:, t, :], in0=xres[:, t, :], scalar=NR.rs[:, 0:1], in1=gfin[:, :],
                                                                  op0=ALU.mult, op1=ALU.mult), reads=[xresb, NR.rsb, gfinb], pwrites=[xresb])
            kb.store(y[t0 + t * 128:t0 + (t + 1) * 128, :], xres[:, t, :], xresb, yb, xsem)
    return kb.finish()


_NC_CACHE = {}


def _get_nc(key, fn):
    if key not in _NC_CACHE:
        _NC_CACHE[key] = fn()
    return _NC_CACHE[key]


W_NAMES = ("w_in", "w_uq", "w_ukv", "w_pa", "w_pb", "w_pc", "w_o", "w_xq", "w_xk", "w_xv", "w_xo", "w_up", "w_down")


def kernel(**inp):
    inp = {k: np.asarray(v) for k, v in inp.items()}
    S, NQ = 16384, 4096
    wb = dict(zip(W_NAMES, cast_weights([inp[k] for k in W_NAMES])))
    p128, p64, ident, ones = perm_consts()
    aconst = attn_consts(S)
    tabs = [rope_tables(np.arange(qt * NQ, (qt + 1) * NQ)) for qt in range(4)]
    xcur = np.ascontiguousarray(inp["x"].reshape(8, NQ, 2048))
    for l in range(2):
        WA = prep_A_weights(wb, l)
        ncA = _get_nc("A", lambda: build_A(TOK=NQ, TB=1024))
        common = {"gmix_b": gbcast(inp["g_mix"][l]), "gcq": gcol(inp["g_cq"][l], 12), "gckv": gcol(inp["g_ckv"][l], 4),
                  "p128": p128, "p64": p64, "ident": ident, "ones": ones}
        for k in ("w_qk", "w_v", "w_lat", "w_uqn", "w_uqr", "w_ukn", "w_ukvv"):
            common[k] = WA[k]
        ims = []
        for c in range(8):
            cH, sH, cR, sR = tabs[c % 4]
            ims.append({**common, "x": xcur[c], "cosH": cH, "sinH": sH, "cosR": cR, "sinR": sR})
        rA = run_spmd(ncA, ims)
        def cat_b(name, b, axis):
            return np.concatenate([rA[4 * b + qt][name] for qt in range(4)], axis=axis)
        ims = []
        for b in range(2):
            QK = cat_b("qkT", b, 2)
            VAB = cat_b("vab", b, 0)
            KM = cat_b("kmean", b, 2)
            QN = cat_b("qnT", b, 2)
            QR = cat_b("qrT", b, 2)
            KN = cat_b("knT", b, 2)
            KR = cat_b("krT", b, 1)
            VC = cat_b("vc", b, 0)
            kr2 = np.ascontiguousarray(np.concatenate([KR, KR], 0))
            for c4 in range(4):
                ims.append({
                    "qn": np.ascontiguousarray(QN[2 * c4:2 * c4 + 2]), "qr": np.ascontiguousarray(QR[c4]),
                    "kn": np.ascontiguousarray(KN[2 * c4:2 * c4 + 2]), "kr2": kr2,
                    "vc": np.ascontiguousarray(VC[:, 256 * c4:256 * c4 + 256]),
                    "qa": np.ascontiguousarray(QK[c4]), "ka": np.ascontiguousarray(QK[4 + c4]),
                    "va": np.ascontiguousarray(VAB[:, 128 * c4:128 * c4 + 128]),
                    "kmean": np.ascontiguousarray(KM[:, c4, :]),
                    "qd": np.ascontiguousarray(np.stack([QK[8 + 4 * g + c4] for g in range(3)])),
                    "kd": np.ascontiguousarray(np.stack([QK[20 + 4 * g + c4] for g in range(3)])),
                    "vd": np.ascontiguousarray(np.concatenate([VAB[:, 512 + (4 * g + c4) * 128:512 + (4 * g + c4 + 1) * 128] for g in range(3)], 1)),
                    **aconst})
        del rA
        ncB = _get_nc("B", lambda: build_B(S=S))
        rB = run_spmd(ncB, ims)
        ncC = _get_nc("C", lambda: build_C(TOK=NQ, TB=512))
        commonC = {"gmix_b": gbcast(inp["g_mix"][l]), "gmem_b": gbcast(inp["g_mem"][l]), "gmkv_b": gbcast(inp["g_memkv"][l]),
                   "w_gates": WA["w_gates"], "w_pa": wb["w_pa"][l], "w_pb": wb["w_pb"][l], "w_pc": wb["w_pc"][l], "w_o": wb["w_o"][l],
                   "w_xq": wb["w_xq"][l], "w_xk": wb["w_xk"][l], "w_xv": wb["w_xv"][l], "w_xo": wb["w_xo"][l],
                   "ident": ident, "ones": ones}
        ims = []
        for c in range(8):
            b, qt = c // 4, c % 4
            sl = slice(qt * NQ, (qt + 1) * NQ)
            ims.append({**commonC, "x": xcur[c], "mem": np.ascontiguousarray(inp["mem"][b]),
                        "oaT": np.ascontiguousarray(np.stack([rB[4 * b + c4]["oaT"][:, sl] for c4 in range(4)])),
                        "obT": np.ascontiguousarray(np.stack([rB[4 * b + c4]["obT"][:, sl] for c4 in range(4)])),
                        "ocT": np.ascontiguousarray(np.stack([rB[4 * b + h // 2]["ocT"][h % 2][:, sl] for h in range(8)]))})
        del rB
        rC = run_spmd(ncC, ims)
        final = l == 1
        ncD = _get_nc("Df" if final else "D", lambda: build_D(TOK=NQ, TB=512, final=final))
        commonD = {"gffn_b": gbcast(inp["g_ffn"][l]), "w_up": wb["w_up"][l], "w_down": wb["w_down"][l],
                   "cw": np.ascontiguousarray(inp["conv_w"][l].T.reshape(86, 128, 3).transpose(1, 0, 2)),
                   "cb": np.ascontiguousarray(inp["conv_b"][l].reshape(86, 128).T), "ident": ident}
        if final:
            commonD["gfin_b"] = np.ascontiguousarray(np.broadcast_to(inp["g_final"][None, :], (128, 2048)))
        ims = []
        for c in range(8):
            halo = rC[c - 1]["x2"][-2:] if c % 4 else np.zeros((2, 2048), np.float32)
            ims.append({**commonD, "xh": np.ascontiguousarray(np.concatenate([halo, rC[c]["x2"]], 0))})
        del rC
        rD = run_spmd(ncD, ims)
        xcur = np.stack([rD[c]["y"] for c in range(8)])
        del rD
    return np.ascontiguousarray(xcur.reshape(2, S, 2048)).astype(np.float32)
```
